# Optimizing a Trainium2 kernel written in Bass

```python
import math
import jax
import jax.numpy as jnp
from jax import lax
import numpy as np


D_MODEL = 1024
BATCH = 2
SEQ = 16384
DEPTH = 2

HEAD_DIM = 64
BLOCK_Q = 128
ROPE_THETA = 10000.0
LN_EPS = 1e-5
N_SB_HEADS = D_MODEL // (2 * HEAD_DIM)
N_DSA_HEADS = D_MODEL // (2 * HEAD_DIM)
N_IDX_HEADS = 4
IDX_DIM = 64
DSA_TOPK_MAX = 256
N_DIFF_HEADS = D_MODEL // (2 * HEAD_DIM)
DIFF_DIM = HEAD_DIM
D_FF = ((-(-8 * D_MODEL // 3)) + 255) // 256 * 256
SB_W = N_SB_HEADS * HEAD_DIM
DSA_W = N_DSA_HEADS * HEAD_DIM
EVEN_SPLITS = (SB_W, SB_W, SB_W, DSA_W, HEAD_DIM, HEAD_DIM, N_IDX_HEADS * IDX_DIM, IDX_DIM, N_IDX_HEADS)
EVEN_IN = sum(EVEN_SPLITS)
EVEN_OUT = SB_W + DSA_W
DIFF_W = N_DIFF_HEADS * 2 * DIFF_DIM
N_EVEN = (DEPTH + 1) // 2
N_ODD = DEPTH // 2
ALPHA = (2 * DEPTH) ** 0.25
BETA = (8 * DEPTH) ** -0.25

kernel_name = 'hybrid_sb_dsa_diff_block'

F32 = jnp.float32


def layer_norm(x, g, b):
    x32 = x.astype(F32)
    mu = jnp.mean(x32, axis=-1, keepdims=True)
    var = jnp.mean(jnp.square(x32 - mu), axis=-1, keepdims=True)
    return ((x32 - mu) * lax.rsqrt(var + LN_EPS) * g.astype(F32) + b.astype(F32)).astype(x.dtype)


def rms_norm(x, g):
    x32 = x.astype(F32)
    ms = jnp.mean(jnp.square(x32), axis=-1, keepdims=True)
    return (x32 * lax.rsqrt(ms + LN_EPS) * g.astype(F32)).astype(x.dtype)


def rope_tables(positions, dim):
    inv = ROPE_THETA ** (-jnp.arange(0, dim, 2, dtype=F32) / dim)
    ang = positions.astype(F32)[..., None] * inv
    return jnp.cos(ang)[:, :, None, :], jnp.sin(ang)[:, :, None, :]


def apply_rope(t, cos, sin):
    t32 = t.astype(F32)
    half = t.shape[-1] // 2
    t1, t2 = t32[..., :half], t32[..., half:]
    return jnp.concatenate([t1 * cos - t2 * sin, t2 * cos + t1 * sin], axis=-1).astype(t.dtype)


def to_blocks(t):
    b, s = t.shape[:2]
    return t.reshape((b, s // BLOCK_Q, BLOCK_Q) + t.shape[2:]).swapaxes(0, 1)


def from_blocks(o):
    nb, b, bq = o.shape[:3]
    return o.swapaxes(0, 1).reshape((b, nb * bq) + o.shape[3:])


def stick_breaking_attention(q, k, v):
    s_len, d = q.shape[1], q.shape[-1]
    scale = d ** -0.5
    key_pos = jnp.arange(s_len)

    def block(args):
        qb, start = args
        z = jnp.einsum('bqhd,bshd->bhqs', qb, k, preferred_element_type=F32) * scale
        qpos = start + jnp.arange(BLOCK_Q)
        past = key_pos[None, :] < qpos[:, None]
        log_keep = jnp.where(past, jax.nn.log_sigmoid(-z), 0.0)
        suffix = lax.cumsum(log_keep, axis=3, reverse=True) - log_keep
        w = jnp.where(past, jnp.exp(jax.nn.log_sigmoid(z) + suffix), 0.0)
        return jnp.einsum('bhqs,bshd->bqhd', w.astype(v.dtype), v)

    starts = jnp.arange(s_len // BLOCK_Q) * BLOCK_Q
    return from_blocks(lax.map(block, (to_blocks(q), starts)))


def dsa_attention(q, k, v, q_idx, k_idx, w_idx):
    s_len, d = q.shape[1], q.shape[-1]
    topk = min(DSA_TOPK_MAX, s_len // 4)
    key_pos = jnp.arange(s_len)
    gather = jax.vmap(lambda t, i: t[i])

    def block(args):
        qb, qib, wb, start = args
        qpos = start + jnp.arange(BLOCK_Q)
        causal = key_pos[None, :] <= qpos[:, None]
        logits = jnp.einsum('bqhe,bse->bhqs', qib, k_idx, preferred_element_type=F32) * IDX_DIM ** -0.5
        score = jnp.einsum('bhqs,bqh->bqs', jax.nn.relu(logits), wb.astype(F32))
        score = jnp.where(causal[None], score, -jnp.inf)
        _, sel = lax.top_k(score, topk)
        valid = sel <= qpos[None, :, None]
        kg = gather(k, sel)
        vg = gather(v, sel)
        s = jnp.einsum('bqhd,bqkd->bhqk', qb, kg, preferred_element_type=F32) * d ** -0.5
        s = jnp.where(valid[:, None], s, -jnp.inf)
        p = jax.nn.softmax(s, axis=-1)
        return jnp.einsum('bhqk,bqkd->bqhd', p.astype(vg.dtype), vg)

    starts = jnp.arange(s_len // BLOCK_Q) * BLOCK_Q
    return from_blocks(lax.map(block, (to_blocks(q), to_blocks(q_idx), to_blocks(w_idx), starts)))


def diff_attention(q, k, v, lam):
    s_len, d = q.shape[1], q.shape[-1]
    key_pos = jnp.arange(s_len)

    def block(args):
        qb, start = args
        qpos = start + jnp.arange(BLOCK_Q)
        causal = key_pos[None, :] <= qpos[:, None]
        s = jnp.einsum('bqhcd,bshcd->bhcqs', qb, k, preferred_element_type=F32) * d ** -0.5
        p = jax.nn.softmax(jnp.where(causal, s, -jnp.inf), axis=-1)
        a = p[:, :, 0] - lam * p[:, :, 1]
        return jnp.einsum('bhqs,bshe->bqhe', a.astype(v.dtype), v)

    starts = jnp.arange(s_len // BLOCK_Q) * BLOCK_Q
    return from_blocks(lax.map(block, (to_blocks(q), starts)))


def even_mixer(h, cos, sin, w_in, w_out):
    b, s, _ = h.shape
    idx, acc = [], 0
    for width in EVEN_SPLITS[:-1]:
        acc += width
        idx.append(acc)
    q_sb, k_sb, v_sb, q_dsa, k_dsa, v_dsa, q_ix, k_ix, w_ix = jnp.split(h @ w_in, idx, axis=-1)
    o_sb = stick_breaking_attention(q_sb.reshape(b, s, N_SB_HEADS, HEAD_DIM),
                                    k_sb.reshape(b, s, N_SB_HEADS, HEAD_DIM),
                                    v_sb.reshape(b, s, N_SB_HEADS, HEAD_DIM))
    o_dsa = dsa_attention(apply_rope(q_dsa.reshape(b, s, N_DSA_HEADS, HEAD_DIM), cos, sin),
                          apply_rope(k_dsa[:, :, None], cos, sin)[:, :, 0],
                          v_dsa,
                          apply_rope(q_ix.reshape(b, s, N_IDX_HEADS, IDX_DIM), cos, sin),
                          apply_rope(k_ix[:, :, None], cos, sin)[:, :, 0],
                          w_ix * N_IDX_HEADS ** -0.5)
    o = jnp.concatenate([o_sb.reshape(b, s, SB_W), o_dsa.reshape(b, s, DSA_W)], axis=-1)
    return o @ w_out


def odd_mixer(h, cos, sin, w_in, lam_q1, lam_k1, lam_q2, lam_k2, subln_g, w_out, lambda_init):
    b, s, _ = h.shape
    q, k, v = jnp.split(h @ w_in, 3, axis=-1)
    q = apply_rope(q.reshape(b, s, 2 * N_DIFF_HEADS, DIFF_DIM), cos, sin).reshape(b, s, N_DIFF_HEADS, 2, DIFF_DIM)
    k = apply_rope(k.reshape(b, s, 2 * N_DIFF_HEADS, DIFF_DIM), cos, sin).reshape(b, s, N_DIFF_HEADS, 2, DIFF_DIM)
    v = v.reshape(b, s, N_DIFF_HEADS, 2 * DIFF_DIM)
    lam = (jnp.exp(jnp.sum(lam_q1.astype(F32) * lam_k1.astype(F32)))
           - jnp.exp(jnp.sum(lam_q2.astype(F32) * lam_k2.astype(F32))) + lambda_init)
    o = diff_attention(q, k, v, lam)
    o = rms_norm(o, subln_g) * (1.0 - lambda_init)
    return o.reshape(b, s, DIFF_W) @ w_out


def swiglu(h, w_gate, w_up, w_down):
    return (jax.nn.silu(h @ w_gate) * (h @ w_up)) @ w_down


def setup_inputs(seed: int = 0) -> dict:
    key = jax.random.key(seed)
    ks = jax.random.split(key, 24)

    def nrm(i, shape, scale):
        return jax.random.normal(ks[i], shape, F32) * scale

    return {
        'x': nrm(0, (BATCH, SEQ, D_MODEL), 1.0),
        'c': nrm(1, (BATCH, D_MODEL), 1.0),
        'positions': jnp.broadcast_to(jnp.arange(SEQ, dtype=jnp.int32), (BATCH, SEQ)),
        'w_mod': nrm(2, (DEPTH, D_MODEL, 6 * D_MODEL), 0.1 * D_MODEL ** -0.5),
        'b_mod': nrm(3, (DEPTH, 6 * D_MODEL), 0.01),
        'w_in_even': nrm(4, (N_EVEN, D_MODEL, EVEN_IN), D_MODEL ** -0.5),
        'w_out_even': nrm(5, (N_EVEN, EVEN_OUT, D_MODEL), BETA * EVEN_OUT ** -0.5),
        'w_in_odd': nrm(6, (N_ODD, D_MODEL, 3 * DIFF_W), D_MODEL ** -0.5),
        'lam_q1': nrm(7, (N_ODD, DIFF_DIM), 0.1),
        'lam_k1': nrm(8, (N_ODD, DIFF_DIM), 0.1),
        'lam_q2': nrm(9, (N_ODD, DIFF_DIM), 0.1),
        'lam_k2': nrm(10, (N_ODD, DIFF_DIM), 0.1),
        'subln_g': 1.0 + nrm(11, (N_ODD, 2 * DIFF_DIM), 0.02),
        'w_out_odd': nrm(12, (N_ODD, DIFF_W, D_MODEL), BETA * DIFF_W ** -0.5),
        'ln_mix_g': 1.0 + nrm(13, (DEPTH, D_MODEL), 0.02),
        'ln_mix_b': nrm(14, (DEPTH, D_MODEL), 0.02),
        'w_gate': nrm(15, (DEPTH, D_MODEL, D_FF), D_MODEL ** -0.5),
        'w_up': nrm(16, (DEPTH, D_MODEL, D_FF), D_MODEL ** -0.5),
        'w_down': nrm(17, (DEPTH, D_FF, D_MODEL), BETA * D_FF ** -0.5),
        'ln_ffn_g': 1.0 + nrm(18, (DEPTH, D_MODEL), 0.02),
        'ln_ffn_b': nrm(19, (DEPTH, D_MODEL), 0.02),
    }


def reference(x, c, positions, w_mod, b_mod, w_in_even, w_out_even, w_in_odd, lam_q1, lam_k1,
              lam_q2, lam_k2, subln_g, w_out_odd, ln_mix_g, ln_mix_b, w_gate, w_up, w_down,
              ln_ffn_g, ln_ffn_b):
    cos, sin = rope_tables(positions, HEAD_DIM)
    mod = jnp.einsum('bd,ldm->lbm', jax.nn.silu(c), w_mod) + b_mod[:, None, :]
    for i in range(DEPTH):
        sh_m, sc_m, g_m, sh_f, sc_f, g_f = [t[:, None, :] for t in jnp.split(mod[i], 6, axis=-1)]
        h = x * (1.0 + sc_m) + sh_m
        if i % 2 == 0:
            y = even_mixer(h, cos, sin, w_in_even[i // 2], w_out_even[i // 2])
        else:
            j = i // 2
            lambda_init = 0.8 - 0.6 * math.exp(-0.3 * i)
            y = odd_mixer(h, cos, sin, w_in_odd[j], lam_q1[j], lam_k1[j], lam_q2[j], lam_k2[j],
                          subln_g[j], w_out_odd[j], lambda_init)
        x = layer_norm(ALPHA * x + (1.0 + g_m) * y, ln_mix_g[i], ln_mix_b[i])
        h = x * (1.0 + sc_f) + sh_f
        y = swiglu(h, w_gate[i], w_up[i], w_down[i])
        x = layer_norm(ALPHA * x + (1.0 + g_f) * y, ln_ffn_g[i], ln_ffn_b[i])
    return x
```

```python
import math
import numpy as np
from contextlib import ExitStack
import concourse.bass as bass
import concourse.mybir as mybir
from concourse.bass_utils import run_bass_kernel_spmd

F32 = mybir.dt.float32
BF16 = mybir.dt.bfloat16
I32 = mybir.dt.int32
AF = mybir.ActivationFunctionType
ALU = mybir.AluOpType
AX = mybir.AxisListType

D = 1024
DFF = 2816
NFF = DFF // 128
ALPHA = 4.0 ** 0.25
LN_EPS = 1e-5
TOPK = 256.0
LAMBDA_INIT = 0.8 - 0.6 * math.exp(-0.3 * 1)
C1 = 6.28125
C2 = 2.0 * math.pi - 6.28125
NEG = -1.0e30


class TK:
    NSLOT = 24
    SEM_MAX = 30000

    def __init__(self, nc, es):
        self.nc = nc
        self.es = es
        self.engs = {'pe': nc.tensor, 'act': nc.scalar, 'dve': nc.vector, 'pool': nc.gpsimd, 'sp': nc.sync}
        self.sems = {}
        self.cnt = {}
        self.cur = {}
        self.gen = {}
        for k in self.engs:
            self.gen[k] = 0
            self._newsem(k)
        self.seen = {k: {} for k in self.engs}
        self.res = {}
        self.slots = {}
        self.slot_rr = {}
        for q in ('sp', 'pool'):
            self.slots[q] = []
            for i in range(self.NSLOT):
                key = "dq_%s_%d" % (q, i)
                self.sems[key] = es.enter_context(nc.semaphore(key))
                self.cnt[key] = 0
                self.slots[q].append(key)
            self.slot_rr[q] = 0
        self.ninst = 0

    def _newsem(self, eng):
        key = "%s#%d" % (eng, self.gen[eng])
        self.gen[eng] += 1
        self.sems[key] = self.es.enter_context(self.nc.semaphore("sem_" + key.replace('#', '_')))
        self.cnt[key] = 0
        self.cur[eng] = key

    def _deps(self, reads, writes):
        deps = {}

        def add(k, v):
            if deps.get(k, 0) < v:
                deps[k] = v
        for r in reads:
            st = self.res.get(r)
            if st is not None and st['w'] is not None:
                add(*st['w'])
        for w in writes:
            st = self.res.get(w)
            if st is not None:
                if st['w'] is not None:
                    add(*st['w'])
                for k, v in st['r'].items():
                    add(k, v)
        return deps

    def _wait(self, eng, deps):
        e = self.engs[eng]
        for k, v in deps.items():
            if eng == 'pe' and k.startswith('pe#'):
                continue
            if self.seen[eng].get(k, 0) >= v:
                continue
            e.wait_ge(self.sems[k], v)
            self.seen[eng][k] = v
            self.ninst += 1

    def _commit(self, tok, reads, writes):
        for r in reads:
            st = self.res.setdefault(r, {'w': None, 'r': {}})
            if st['r'].get(tok[0], 0) < tok[1]:
                st['r'][tok[0]] = tok[1]
        for w in writes:
            self.res[w] = {'w': tok, 'r': {}}

    def op(self, eng, fn, reads=(), writes=()):
        self._wait(eng, self._deps(reads, writes))
        inst = fn(self.engs[eng])
        if self.cnt[self.cur[eng]] >= self.SEM_MAX:
            self._newsem(eng)
        key = self.cur[eng]
        self.cnt[key] += 1
        inst.then_inc(self.sems[key], 1)
        tok = (key, self.cnt[key])
        self._commit(tok, reads, writes)
        self.ninst += 1
        return tok

    def dma(self, q, out, in_, reads=(), writes=(), **kw):
        deps = self._deps(reads, writes)
        slot = self.slots[q][self.slot_rr[q] % len(self.slots[q])]
        self.slot_rr[q] += 1
        if self.cnt[slot] > 0:
            deps[slot] = max(deps.get(slot, 0), self.cnt[slot])
        assert self.cnt[slot] < 60000
        self._wait(q, deps)
        inst = self.engs[q].dma_start(out=out, in_=in_, **kw)
        self.cnt[slot] += 16
        inst.then_inc(self.sems[slot], 16)
        tok = (slot, self.cnt[slot])
        self._commit(tok, reads, writes)
        self.ninst += 1
        return tok

    def finish(self, eng='sp'):
        e = self.engs[eng]
        for k, v in self.cnt.items():
            if v > 0 and self.seen[eng].get(k, 0) < v:
                e.wait_ge(self.sems[k], v)
                self.seen[eng][k] = v


class Ring:
    def __init__(self, items):
        self.items = items
        self.i = 0

    def next(self):
        it = self.items[self.i % len(self.items)]
        self.i += 1
        return it


def own_tiles(j, NT):
    return sorted([8 * m + j for m in range(NT // 8)] + [8 * m + 7 - j for m in range(NT // 8)])


class Ctx:
    def __init__(self, S):
        self.S = S
        self.NT = S // 512
        self.NO = self.NT // 4
        self.T = self.NO * 512
        self.NKB = S // 128
        self.nc = bass.Bass("TRN2", target_bir_lowering=False)
        self.es = ExitStack()
        self.tk = TK(self.nc, self.es)
        self.uid = 0

    def din(self, name, shape, dt):
        return self.nc.dram_tensor(name, list(shape), dt, kind="ExternalInput").ap()

    def dout(self, name, shape, dt):
        return self.nc.dram_tensor(name, list(shape), dt, kind="ExternalOutput").ap()

    def dscr(self, name, shape, dt, dbg=False):
        return self.nc.dram_tensor(name, list(shape), dt, kind=("ExternalOutput" if dbg else "Internal")).ap()

    def sb(self, es, name, shape, dt):
        self.uid += 1
        return es.enter_context(self.nc.sbuf_tensor("s%d_%s" % (self.uid, name), list(shape), dt))

    def ps(self, es, name, shape, dt):
        return es.enter_context(self.nc.psum_tensor("p_" + name, list(shape), dt))


def make_consts(cx, es, cdram):
    nc, tk = cx.nc, cx.tk
    K = {}
    K['ident'] = cx.sb(es, "ident", [128, 128], BF16)
    K['identf'] = cx.sb(es, "identf", [128, 128], F32)
    K['uincl'] = cx.sb(es, "uincl", [128, 128], BF16)
    K['lstr'] = cx.sb(es, "lstr", [128, 128], BF16)
    K['ones'] = cx.sb(es, "ones", [128, 128], F32)
    K['onesb'] = cx.sb(es, "onesb", [128, 128], BF16)
    K['cbias'] = cx.sb(es, "cbias", [128, 128], F32)
    K['inv'] = cx.sb(es, "invc", [128, 1], F32)
    K['sgn'] = cx.sb(es, "sgnc", [128, 1], F32)
    K['halfpi'] = cx.sb(es, "halfpi", [128, 1], F32)
    K['eps'] = cx.sb(es, "epsc", [128, 1], F32)
    tk.dma('sp', K['inv'][:], cdram['inv'][:, :], writes=['c_inv'])
    tk.dma('sp', K['sgn'][:], cdram['sgn'][:, :], writes=['c_sgn'])
    g = 'pool'
    tk.op(g, lambda e: e.memset(K['halfpi'][:], float(math.pi / 2)), [], ['c_halfpi'])
    tk.op(g, lambda e: e.memset(K['eps'][:], LN_EPS), [], ['c_eps'])
    tk.op(g, lambda e: e.memset(K['ones'][:], 1.0), [], ['c_ones'])
    tk.op(g, lambda e: e.memset(K['onesb'][:], 1.0), [], ['c_onesb'])
    for nm, op_, fill in (('ident', ALU.is_equal, 0.0), ('uincl', ALU.is_ge, 0.0)):
        tk.op(g, lambda e: e.memset(K[nm][:], 1.0), [], ['c_' + nm])
        tk.op(g, lambda e: e.affine_select(out=K[nm][:], in_=K[nm][:], pattern=[[-1, 128]], compare_op=op_,
                                           fill=fill, base=0, channel_multiplier=1), ['c_' + nm], ['c_' + nm])
    tk.op(g, lambda e: e.memset(K['identf'][:], 1.0), [], ['c_identf'])
    tk.op(g, lambda e: e.affine_select(out=K['identf'][:], in_=K['identf'][:], pattern=[[-1, 128]],
                                       compare_op=ALU.is_equal, fill=0.0, base=0, channel_multiplier=1),
          ['c_identf'], ['c_identf'])
    tk.op(g, lambda e: e.memset(K['lstr'][:], 1.0), [], ['c_lstr'])
    tk.op(g, lambda e: e.affine_select(out=K['lstr'][:], in_=K['lstr'][:], pattern=[[1, 128]], compare_op=ALU.is_ge,
                                       fill=0.0, base=-1, channel_multiplier=-1), ['c_lstr'], ['c_lstr'])
    tk.op(g, lambda e: e.memset(K['cbias'][:], 0.0), [], ['c_cbias'])
    tk.op(g, lambda e: e.affine_select(out=K['cbias'][:], in_=K['cbias'][:], pattern=[[-1, 128]], compare_op=ALU.is_ge,
                                       fill=NEG, base=0, channel_multiplier=1), ['c_cbias'], ['c_cbias'])
    return K


def compute_mod(cx, es_outer, K, PSB, c_col_d, w_mod_d, b_mod_d, tag, want_cols=True, want_bcast=True):
    nc, tk = cx.nc, cx.tk
    modc = cx.sb(es_outer, "modc" + tag, [128, 48], F32) if want_cols else None
    gbm = cx.sb(es_outer, "gbm" + tag, [128, 1024], F32) if want_bcast else None
    gbf = cx.sb(es_outer, "gbf" + tag, [128, 1024], F32) if want_bcast else None
    with ExitStack() as es:
        if not want_cols:
            modc = cx.sb(es, "modc_tmp" + tag, [128, 48], F32)
        if not want_bcast:
            gbm = cx.sb(es, "gbm_tmp" + tag, [128, 1024], F32)
            gbf = cx.sb(es, "gbf_tmp" + tag, [128, 1024], F32)
        ccol = cx.sb(es, "ccol" + tag, [128, 8], F32)
        scol = cx.sb(es, "scol" + tag, [128, 8], F32)
        row = cx.sb(es, "mrow" + tag, [1, 6144], F32)
        brow = cx.sb(es, "brow" + tag, [1, 6144], F32)
        wbuf = [cx.sb(es, "wmod%s%d" % (tag, i), [128, 8, 512], F32) for i in range(2)]
        tk.dma('sp', ccol[:], c_col_d[:, :], writes=['ccol' + tag])
        tk.dma('sp', brow[:], b_mod_d[0:1, :], writes=['brow' + tag])
        tk.op('act', lambda e: e.activation(out=scol[:], in_=ccol[:], func=AF.Silu), ['ccol' + tag], ['scol' + tag])
        for gidx in range(12):
            wb = wbuf[gidx % 2]
            wk = 'wmod%s%d' % (tag, gidx % 2)
            tk.dma('sp', wb[:], w_mod_d[:, gidx * 512:(gidx + 1) * 512].rearrange("(c p) n -> p c n", p=128),
                   writes=[wk])
            pb = PSB[gidx % 2]
            pk = 'psb%d' % (gidx % 2)
            for k in range(8):
                tk.op('pe', lambda e: e.matmul(pb[0:1, :], lhsT=scol[:, k:k + 1], rhs=wb[:, k, :], start=(k == 0),
                                               stop=(k == 7)), [wk, 'scol' + tag], [pk])
            tk.op('dve', lambda e: e.tensor_tensor(out=row[0:1, gidx * 512:(gidx + 1) * 512], in0=pb[0:1, :],
                                                   in1=brow[0:1, gidx * 512:(gidx + 1) * 512], op=ALU.add),
                  [pk, 'brow' + tag], ['mrow' + tag])
        for ch in (1, 2, 4, 5):
            tk.op('dve', lambda e: e.tensor_scalar(out=row[0:1, ch * 1024:(ch + 1) * 1024],
                                                   in0=row[0:1, ch * 1024:(ch + 1) * 1024], scalar1=1.0, scalar2=None,
                                                   op0=ALU.add), ['mrow' + tag], ['mrow' + tag])
        pb = PSB[2]
        for c in range(48):
            tk.op('pe', lambda e: e.matmul(pb[:, c:c + 1], lhsT=row[0:1, c * 128:(c + 1) * 128], rhs=K['ones'][0:1, 0:1],
                                           start=True, stop=True), ['mrow' + tag, 'c_ones'], ['psb2'])
        tk.op('dve', lambda e: e.tensor_copy(out=modc[:], in_=pb[:, 0:48]), ['psb2'], ['modc' + tag])
        for (dst, ch, nm) in ((gbm, 2, 'gbm'), (gbf, 5, 'gbf')):
            for hlf in range(2):
                pb = PSB[3 + hlf]
                pk = 'psb%d' % (3 + hlf)
                tk.op('pe', lambda e: e.matmul(pb[:, :], lhsT=K['ones'][0:1, 0:128],
                                               rhs=row[0:1, ch * 1024 + hlf * 512: ch * 1024 + (hlf + 1) * 512],
                                               start=True, stop=True), ['mrow' + tag, 'c_ones'], [pk])
                tk.op('dve', lambda e: e.tensor_copy(out=dst[:, hlf * 512:(hlf + 1) * 512], in_=pb[:, :]), [pk],
                      [nm + tag])
        drain(cx, K)
    return modc, gbm, gbf


def rope_tables(cx, K, posf_t, posf_key, cosF, sinS, keyc, keys_, tmp):
    tk = cx.tk
    ang, ki, kf, r = tmp['ang'], tmp['ki'], tmp['kf'], tmp['r']
    tk.op('dve', lambda e: e.tensor_scalar(out=ang[:], in0=posf_t, scalar1=K['inv'][:, 0:1], scalar2=None, op0=ALU.mult),
          [posf_key, 'c_inv'], ['rt_ang'])
    tk.op('dve', lambda e: e.tensor_scalar(out=ki[:], in0=ang[:], scalar1=float(1.0 / (2 * math.pi)), scalar2=None,
                                           op0=ALU.mult), ['rt_ang'], ['rt_ki'])
    tk.op('dve', lambda e: e.tensor_copy(out=kf[:], in_=ki[:]), ['rt_ki'], ['rt_kf'])
    tk.op('dve', lambda e: e.scalar_tensor_tensor(out=r[:], in0=kf[:], scalar=float(-C1), in1=ang[:], op0=ALU.mult,
                                                  op1=ALU.add), ['rt_kf', 'rt_ang'], ['rt_r'])
    tk.op('dve', lambda e: e.scalar_tensor_tensor(out=ang[:], in0=kf[:], scalar=float(-C2), in1=r[:], op0=ALU.mult,
                                                  op1=ALU.add), ['rt_kf', 'rt_r'], ['rt_ang'])
    tk.op('dve', lambda e: e.tensor_scalar(out=r[:], in0=ang[:], scalar1=float(math.pi), scalar2=float(-math.pi),
                                           op0=ALU.min, op1=ALU.max), ['rt_ang'], ['rt_r'])
    tk.op('act', lambda e: e.activation(out=sinS, in_=r[:], func=AF.Sin, scale=K['sgn'][:, 0:1]), ['rt_r', 'c_sgn'],
          [keys_])
    tk.op('dve', lambda e: e.scalar_tensor_tensor(out=ang[:], in0=r[:], scalar=-1.0, in1=r[:], op0=ALU.mult,
                                                  op1=ALU.max), ['rt_r'], ['rt_ang'])
    tk.op('act', lambda e: e.activation(out=cosF, in_=ang[:], func=AF.Sin, scale=-1.0, bias=K['halfpi'][:, 0:1]),
          ['rt_ang', 'c_halfpi'], [keyc])


def phase_proj0(cx, K, BANK, din, scr, modc):
    nc, tk = cx.nc, cx.tk
    NT = cx.NT
    with ExitStack() as es:
        wK = cx.sb(es, "wK", [128, 8, 1024], BF16)
        wQ = cx.sb(es, "wQ", [128, 8, 2048], BF16)
        wV = cx.sb(es, "wV", [128, 8, 68 + 512], BF16)
        for c in range(8):
            tk.dma('pool', wK[:, c, :], din['wK'][c * 128:(c + 1) * 128, :], writes=['wK'])
            tk.dma('pool', wQ[:, c, :], din['wQ'][c * 128:(c + 1) * 128, :], writes=['wQ'])
            tk.dma('pool', wV[:, c, :], din['wV'][c * 128:(c + 1) * 128, :], writes=['wV'])
        xts = Ring([(cx.sb(es, "xt%d" % i, [128, 8, 512], F32), 'xt%d' % i) for i in range(2)])
        hts = Ring([(cx.sb(es, "ht%d" % i, [128, 8, 512], BF16), 'ht%d' % i) for i in range(2)])
        posi = cx.sb(es, "posi", [128, 512], I32)
        posf = cx.sb(es, "posf", [128, 512], F32)
        tabs = Ring([(cx.sb(es, "cosF%d" % i, [128, 512], F32), cx.sb(es, "sinS%d" % i, [128, 512], F32), i)
                     for i in range(2)])
        tmp = {'ang': cx.sb(es, "rt_ang", [128, 512], F32), 'ki': cx.sb(es, "rt_ki", [128, 512], I32),
               'kf': cx.sb(es, "rt_kf", [128, 512], F32), 'r': cx.sb(es, "rt_r", [128, 512], F32)}
        stg = Ring([(cx.sb(es, "stg%d" % i, [128, 512], BF16), 'stg%d' % i) for i in range(4)])
        t1s = Ring([(cx.sb(es, "ropet1_%d" % i, [128, 512], F32), 'ropet1_%d' % i) for i in range(2)])
        t2s = Ring([(cx.sb(es, "ropet2_%d" % i, [128, 512], F32), 'ropet2_%d' % i) for i in range(2)])
        vstg = Ring([(cx.sb(es, "vstg%d" % i, [128, 512], BF16), 'vstg%d' % i) for i in range(2)])
        vdst = Ring([(cx.sb(es, "vdst%d" % i, [128, 64], BF16), 'vdst%d' % i) for i in range(2)])
        wxst = Ring([(cx.sb(es, "wxst%d" % i, [128, 4], F32), 'wxst%d' % i) for i in range(2)])
        kbanks = Ring([0, 1, 2, 3])
        kpairs = Ring([(0, 1), (2, 3)])
        vbanks = Ring([4, 5, 6, 7])
        evq = Ring(['act', 'dve'])

        def group(W, col0, ht, hk, wkey, bank):
            for c in range(8):
                tk.op('pe', lambda e: e.matmul(BANK[bank][:, :], lhsT=W[:, c, col0:col0 + 128], rhs=ht[:, c, :],
                                               start=(c == 0), stop=(c == 7)), [wkey, hk], ['bank%d' % bank])

        def plain(W, col0, ht, hk, wkey, dst):
            b = kbanks.next()
            group(W, col0, ht, hk, wkey, b)
            st, sk = stg.next()
            tk.op('act', lambda e: e.copy(out=st[:], in_=BANK[b][:, :]), ['bank%d' % b], [sk])
            tk.dma('pool', dst, st[:], reads=[sk])

        def roped(W, col0, colr, ht, hk, wkey, dst, cosF, sinS, tkey):
            b0, b1 = kpairs.next()
            group(W, col0, ht, hk, wkey, b0)
            group(W, colr, ht, hk, wkey, b1)
            t1, k1 = t1s.next()
            t2, k2 = t2s.next()
            tk.op('dve', lambda e: e.tensor_tensor(out=t1[:], in0=BANK[b0][:, :], in1=cosF[:], op=ALU.mult),
                  ['bank%d' % b0, 'cos' + tkey], [k1])
            tk.op('dve', lambda e: e.tensor_tensor(out=t2[:], in0=BANK[b1][:, :], in1=sinS[:], op=ALU.mult),
                  ['bank%d' % b1, 'sin' + tkey], [k2])
            st, sk = stg.next()
            tk.op('pool', lambda e: e.tensor_tensor(out=st[:], in0=t1[:], in1=t2[:], op=ALU.add), [k1, k2], [sk])
            tk.dma('pool', dst, st[:], reads=[sk])

        def do_tile(i, which):
            cols = slice(i * 512, (i + 1) * 512)
            xsrc = din['xT'] if which == 'kv' else din['xTown']
            psrc = din['pos'] if which == 'kv' else din['posown']
            xt, xk = xts.next()
            tk.dma('sp', xt[:], xsrc[:, cols].rearrange("(c p) t -> p c t", p=128), writes=[xk])
            tk.dma('sp', posi[:], psrc[0:1, cols].partition_broadcast(128), writes=['posi'])
            tk.op('dve', lambda e: e.tensor_copy(out=posf[:], in_=posi[:]), ['posi'], ['posf'])
            cosF, sinS, ti = tabs.next()
            tkey = str(ti)
            rope_tables(cx, K, posf[:], 'posf', cosF[:], sinS[:], 'cos' + tkey, 'sin' + tkey, tmp)
            ht, hk = hts.next()
            for c in range(8):
                if c % 2 == 0:
                    tk.op('dve', lambda e: e.tensor_scalar(out=ht[:, c, :], in0=xt[:, c, :], scalar1=modc[:, 8 + c:9 + c],
                                                           scalar2=modc[:, c:c + 1], op0=ALU.mult, op1=ALU.add),
                          [xk, 'modc0'], [hk])
                else:
                    tk.op('act', lambda e: e.activation(out=ht[:, c, :], in_=xt[:, c, :], func=AF.Identity,
                                                        scale=modc[:, 8 + c:9 + c], bias=modc[:, c:c + 1]),
                          [xk, 'modc0'], [hk])
            if which == 'kv':
                for c in range(4):
                    plain(wK, c * 128, ht, hk, 'wK', scr['ksbT'][c][:, cols])
                roped(wK, 512, 640, ht, hk, 'wK', scr['kdT'][:, cols], cosF, sinS, tkey)
                roped(wK, 768, 896, ht, hk, 'wK', scr['kixT'][:, cols], cosF, sinS, tkey)
            else:
                for c in range(4):
                    plain(wQ, c * 128, ht, hk, 'wQ', scr['qsbT'][c][:, cols])
                for c in range(4):
                    roped(wQ, 512 + c * 128, 1024 + c * 128, ht, hk, 'wQ', scr['qdT'][c][:, cols], cosF, sinS, tkey)
                for c in range(2):
                    roped(wQ, 1536 + c * 128, 1792 + c * 128, ht, hk, 'wQ', scr['qixT'][c][:, cols], cosF, sinS, tkey)
            for sbk in range(4):
                rows = slice(i * 512 + sbk * 128, i * 512 + (sbk + 1) * 128)
                if which == 'kv':
                    b = vbanks.next()
                    for c in range(8):
                        tk.op('pe', lambda e: e.matmul(BANK[b][:, :], lhsT=ht[:, c, sbk * 128:(sbk + 1) * 128],
                                                       rhs=wV[:, c, 0:512], start=(c == 0), stop=(c == 7)),
                              ['wV', hk], ['bank%d' % b])
                    vs, vk = vstg.next()
                    tk.op('act', lambda e: e.copy(out=vs[:], in_=BANK[b][:, :]), ['bank%d' % b], [vk])
                    tk.dma('pool', scr['vsb'][rows, :], vs[:], reads=[vk])
                b = vbanks.next()
                for c in range(8):
                    tk.op('pe', lambda e: e.matmul(BANK[b][:, 0:68], lhsT=ht[:, c, sbk * 128:(sbk + 1) * 128],
                                                   rhs=wV[:, c, 512:580], start=(c == 0), stop=(c == 7)),
                          ['wV', hk], ['bank%d' % b])
                if which == 'kv':
                    vd_, vdk = vdst.next()
                    tk.op('dve', lambda e: e.tensor_copy(out=vd_[:], in_=BANK[b][:, 0:64]), ['bank%d' % b], [vdk])
                    tk.dma('pool', scr['vd'][rows, :], vd_[:], reads=[vdk])
                else:
                    wx, wxk = wxst.next()
                    tk.op('dve', lambda e: e.tensor_scalar(out=wx[:], in0=BANK[b][:, 64:68], scalar1=1.0 / 16.0,
                                                           scalar2=None, op0=ALU.mult), ['bank%d' % b], [wxk])
                    tk.dma('pool', scr['wix'][rows, :], wx[:], reads=[wxk])

        for i in range(NT):
            do_tile(i, 'kv')
        for i in range(cx.NO):
            do_tile(i, 'q')
        drain(cx, K)


def drain(cx, K):
    tk = cx.tk
    for eng in ('pe', 'act', 'dve', 'pool', 'sp'):
        e = tk.engs[eng]
        for k, v in tk.cnt.items():
            if v > 0 and tk.seen[eng].get(k, 0) < v:
                if eng == 'pe' and k.startswith('pe#'):
                    continue
                e.wait_ge(tk.sems[k], v)
                tk.seen[eng][k] = v


def _rot64(w):
    return np.concatenate([w[:, 32:64], w[:, 0:32]], axis=1)


def _rot_heads(w):
    return np.concatenate([_rot64(w[:, h * 64:(h + 1) * 64]) for h in range(w.shape[1] // 64)], axis=1)


def rope_consts():
    p = np.arange(128)
    inv = (np.float32(10000.0) ** (-(np.arange(0, 64, 2, dtype=np.float32)) / np.float32(64))).astype(np.float32)
    invc = inv[p % 32].reshape(128, 1).astype(np.float32)
    sgn = np.where((p % 64) < 32, -1.0, 1.0).astype(np.float32).reshape(128, 1)
    return invc, sgn


def prep_weights_A(inp):
    w = np.asarray(inp['w_in_even'][0], dtype=np.float32)
    ksb = w[:, 512:1024]
    kd = w[:, 2048:2112]
    kix = w[:, 2432:2496]
    wK = np.concatenate([ksb, kd, kd, _rot64(kd), _rot64(kd), kix, kix, _rot64(kix), _rot64(kix)], axis=1)
    qsb = w[:, 0:512]
    qd = w[:, 1536:2048]
    qix = w[:, 2176:2432]
    wQ = np.concatenate([qsb, qd, _rot_heads(qd), qix, _rot_heads(qix)], axis=1)
    wV = np.concatenate([w[:, 1024:1536], w[:, 2112:2176], w[:, 2496:2500]], axis=1)
    wo = np.asarray(inp['w_in_odd'][0], dtype=np.float32)
    q1, k1, v1 = wo[:, 0:1024], wo[:, 1024:2048], wo[:, 2048:3072]
    w1 = np.concatenate([q1, _rot_heads(q1), k1, _rot_heads(k1), v1], axis=1)
    return {k: np.ascontiguousarray(v) for k, v in dict(wK=wK, wQ=wQ, wV=wV, w1=w1).items()}


def build_A(S, stop_after=99, dbg=False):
    cx = Ctx(S)
    nc, tk, es = cx.nc, cx.tk, cx.es
    T = cx.T
    din = {}
    din['xT'] = cx.din("xT", [D, S], F32)
    din['xown'] = cx.din("xown", [T, D], F32)
    din['xTown'] = cx.din("xTown", [D, T], F32)
    din['joff'] = cx.din("joff", [128, 1], F32)
    din['pos'] = cx.din("pos", [1, S], I32)
    din['posown'] = cx.din("posown", [1, T], I32)
    din['ccol'] = cx.din("ccol", [128, 8], F32)
    din['inv'] = cx.din("inv", [128, 1], F32)
    din['sgn'] = cx.din("sgn", [128, 1], F32)
    for l in range(2):
        din['wmod%d' % l] = cx.din("wmod%d" % l, [D, 6 * D], F32)
        din['bmod%d' % l] = cx.din("bmod%d" % l, [1, 6 * D], F32)
    din['wK'] = cx.din("wK", [D, 1024], F32)
    din['wQ'] = cx.din("wQ", [D, 2048], F32)
    din['wV'] = cx.din("wV", [D, 580], F32)
    din['w1'] = cx.din("w1", [D, 5120], F32)
    din['wout'] = cx.din("wout", [D, D], F32)
    din['wgate'] = cx.din("wgate", [D, DFF], F32)
    din['wup'] = cx.din("wup", [D, DFF], F32)
    din['wdown'] = cx.din("wdown", [DFF, D], F32)
    for nm in ('lnmg', 'lnmb', 'lnfg', 'lnfb'):
        din[nm] = cx.din(nm, [1, D], F32)
    scr = {}
    scr['ksbT'] = [cx.dscr("ksbT%d" % c, [128, S], BF16, dbg) for c in range(4)]
    scr['kdT'] = cx.dscr("kdT", [128, S], BF16, dbg)
    scr['kixT'] = cx.dscr("kixT", [128, S], BF16, dbg)
    scr['vsb'] = cx.dscr("vsb", [S, 512], BF16, dbg)
    scr['vd'] = cx.dscr("vd", [S, 64], BF16, dbg)
    scr['qsbT'] = [cx.dscr("qsbT%d" % c, [128, T], BF16, dbg) for c in range(4)]
    scr['qdT'] = [cx.dscr("qdT%d" % c, [128, T], BF16, dbg) for c in range(4)]
    scr['qixT'] = [cx.dscr("qixT%d" % c, [128, T], BF16, dbg) for c in range(2)]
    scr['wix'] = cx.dscr("wix", [T, 4], F32, dbg)
    scr['oT'] = cx.dscr("oT", [D, T], BF16, dbg)
    scr['xmix'] = cx.dscr("xmix", [T, D], F32, dbg)
    dout = {}
    dout['x1'] = cx.dout("x1", [T, D], F32)
    dout['q1T'] = cx.dout("q1T", [D, T], BF16)
    dout['k1T'] = cx.dout("k1T", [D, T], BF16)
    dout['v1'] = cx.dout("v1", [T, D], BF16)
    with es:
        PS = [cx.ps(es, "PS%d" % i, [128, 1024], F32) for i in range(4)]
        BANK = [PS[i // 2][:, (i % 2) * 512:(i % 2 + 1) * 512] for i in range(8)]
        K = make_consts(cx, es, din)
        modc0, _, _ = compute_mod(cx, es, K, BANK, din['ccol'], din['wmod0'], din['bmod0'], '0', want_bcast=False)
        modc1, _, _ = compute_mod(cx, es, K, BANK, din['ccol'], din['wmod1'], din['bmod1'], '1', want_bcast=False)
        drain(cx, K)
        joff_t = cx.sb(es, "joff_t", [128, 1], F32)
        tk.dma('sp', joff_t[:], din['joff'][:, :], writes=['joff'])
        if stop_after >= 1:
            phase_proj0(cx, K, BANK, din, scr, modc0)
        if stop_after >= 2:
            phase_sb(cx, K, BANK, din, scr, joff_t)
        if stop_after >= 3:
            phase_dsa(cx, K, PS, BANK, din, scr, joff_t)
        if stop_after >= 4:
            _, gbm0, gbf0 = compute_mod(cx, es, K, BANK, din['ccol'], din['wmod0'], din['bmod0'], '0b', want_cols=False)
            phase_outproj(cx, K, PS, din, din['wout'], din['lnmg'], din['lnmb'], gbm0, 'gbm0b', scr['oT'], din['xown'],
                          scr['xmix'])
            phase_ffn(cx, K, PS, BANK, din['wgate'], din['wup'], din['wdown'], din['lnfg'], din['lnfb'], modc0, 'modc0',
                      gbf0, 'gbf0b', scr['xmix'], dout['x1'])
        if stop_after >= 5:
            phase_proj1(cx, K, BANK, din, modc1, 'modc1', dout['x1'], dout)
        tk.finish('sp')
    return cx


def own_index(j, S):
    T = S // 4
    k = np.arange(T) // 128
    p = np.arange(T) % 128
    return (4 * k + j) * 128 + p


def prep_A(inp, S):
    x = np.asarray(inp['x'], dtype=np.float32)[:, :S]
    pos = np.asarray(inp['positions'])[:, :S].astype(np.int32)
    c = np.asarray(inp['c'], dtype=np.float32)
    W = prep_weights_A(inp)
    invc, sgn = rope_consts()
    common = dict(inv=invc, sgn=sgn, wK=W['wK'], wQ=W['wQ'], wV=W['wV'], w1=W['w1'],
                  wout=np.ascontiguousarray(inp['w_out_even'][0], dtype=np.float32),
                  wgate=np.ascontiguousarray(inp['w_gate'][0], dtype=np.float32),
                  wup=np.ascontiguousarray(inp['w_up'][0], dtype=np.float32),
                  wdown=np.ascontiguousarray(inp['w_down'][0], dtype=np.float32),
                  lnmg=np.asarray(inp['ln_mix_g'][0:1], dtype=np.float32), lnmb=np.asarray(inp['ln_mix_b'][0:1], dtype=np.float32),
                  lnfg=np.asarray(inp['ln_ffn_g'][0:1], dtype=np.float32), lnfb=np.asarray(inp['ln_ffn_b'][0:1], dtype=np.float32))
    for l in range(2):
        common['wmod%d' % l] = np.ascontiguousarray(inp['w_mod'][l], dtype=np.float32)
        common['bmod%d' % l] = np.ascontiguousarray(inp['b_mod'][l:l + 1], dtype=np.float32)
    maps = []
    for core in range(8):
        b, j = core // 4, core % 4
        oi = own_index(j, S)
        m = dict(common)
        m['xT'] = np.ascontiguousarray(x[b].T)
        m['xown'] = np.ascontiguousarray(x[b][oi])
        m['xTown'] = np.ascontiguousarray(x[b][oi].T)
        m['pos'] = np.ascontiguousarray(pos[b:b + 1])
        m['posown'] = np.ascontiguousarray(pos[b:b + 1][:, oi])
        m['ccol'] = np.ascontiguousarray(c[b].reshape(8, 128).T)
        m['joff'] = np.full((128, 1), 128.0 * j, dtype=np.float32)
        maps.append(m)
    return maps


def make_tmasks(cx, es, joff_t, strict, name):
    tk = cx.tk
    m = cx.sb(es, name, [128, 16, 512], BF16)
    with ExitStack() as es2:
        vi = cx.sb(es2, name + "_vi", [128, 16, 512], I32)
        vf = cx.sb(es2, name + "_vf", [128, 16, 512], F32)
        tk.op('pool', lambda e: e.iota(vi[:], pattern=[[-128, 16], [512, 4], [1, 128]], base=0, channel_multiplier=-1),
              [], [name + '_vi'])
        tk.op('dve', lambda e: e.tensor_copy(out=vf[:], in_=vi[:]), [name + '_vi'], [name + '_vf'])
        tk.op('dve', lambda e: e.tensor_scalar(out=m[:], in0=vf[:], scalar1=joff_t[:, 0:1], scalar2=0.0, op0=ALU.add,
                                               op1=(ALU.is_gt if strict else ALU.is_ge)), [name + '_vf', 'joff'], [name])
        drain(cx, None)
    return m


def phase_sb(cx, K, BANK, din, scr, joff_t):
    nc, tk = cx.nc, cx.tk
    S, T, NO, NKB = cx.S, cx.T, cx.NO, cx.NKB
    with ExitStack() as es:
        mstrict = make_tmasks(cx, es, joff_t, True, "mstr16")
        KT = cx.sb(es, "sbKT", [128, S], BF16)
        V = cx.sb(es, "sbV", [128, NKB, 128], BF16)
        QT = cx.sb(es, "sbQT", [128, T], BF16)
        NB = 2
        bufs = {}
        for ch in range(2):
            for nm, dt in (('E', F32), ('SP', BF16), ('X', BF16), ('W', BF16)):
                bufs[(ch, nm)] = Ring([(cx.sb(es, "sb%s%d_%d" % (nm, ch, i), [128, 512], dt), "sb%s%d_%d" % (nm, ch, i))
                                       for i in range(NB)])
        ost = Ring([(cx.sb(es, "sbost%d" % i, [64, 512], BF16), "sbost%d" % i) for i in range(2)])
        zb = {0: Ring([0, 1]), 1: Ring([4, 5])}
        cb = {0: 2, 1: 6}
        ob = {0: 3, 1: 7}
        for c in range(4):
            tk.dma('sp', KT[:], scr['ksbT'][c][:, :], reads=['d_ksbT'], writes=['sbKT'])
            tk.dma('sp', V[:], scr['vsb'][:, c * 128:(c + 1) * 128].rearrange("(kb p) d -> p kb d", p=128),
                   reads=['d_vsb'], writes=['sbV'])
            tk.dma('sp', QT[:], scr['qsbT'][c][:, :], reads=['d_qsbT'], writes=['sbQT'])
            for i in range(NO):
                nkb = 16 * (i + 1)
                order = list(range(nkb - 1, -1, -1))
                qs = slice(i * 512, (i + 1) * 512)
                st = {}

                def mm_z(ch, kb):
                    pb = 64 * ch
                    b = zb[ch].next()
                    tk.op('pe', lambda e: e.matmul(BANK[b][:, :], lhsT=KT[pb:pb + 64, kb * 128:(kb + 1) * 128],
                                                   rhs=QT[pb:pb + 64, qs], start=True, stop=True),
                          ['sbKT', 'sbQT'], ['bank%d' % b])
                    st[(ch, kb, 'z')] = b

                for ch in range(2):
                    mm_z(ch, order[0])
                for n, kb in enumerate(order):
                    kbr = kb - 16 * i
                    first, last = (n == 0), (n == nkb - 1)
                    for ch in range(2):
                        b = st.pop((ch, kb, 'z'))
                        E, Ek = bufs[(ch, 'E')].next()
                        SP, SPk = bufs[(ch, 'SP')].next()
                        tk.op('act', lambda e: e.activation(out=E[:], in_=BANK[b][:, :], func=AF.Exp, scale=0.125),
                              ['bank%d' % b], [Ek])
                        tk.op('act', lambda e: e.activation(out=SP[:], in_=E[:], func=AF.Ln, bias=K['ones'][:, 0:1]),
                              [Ek, 'c_ones'], [SPk])
                        if kbr >= 0:
                            tk.op('pool', lambda e: e.tensor_tensor(out=SP[:], in0=SP[:], in1=mstrict[:, kbr, :],
                                                                    op=ALU.mult), [SPk, 'mstr16'], [SPk])
                        st[(ch, 'E')] = (E, Ek)
                        st[(ch, 'SP')] = (SP, SPk)
                    for ch in range(2):
                        SP, SPk = st[(ch, 'SP')]
                        tk.op('pe', lambda e: e.matmul(BANK[cb[ch]][:, :], lhsT=K['uincl'][:, :], rhs=SP[:], start=first,
                                                       stop=False), [SPk, 'c_uincl'], ['bank%d' % cb[ch]])
                    if not last:
                        for ch in range(2):
                            mm_z(ch, order[n + 1])
                    for ch in range(2):
                        X, Xk = bufs[(ch, 'X')].next()
                        tk.op('act', lambda e: e.activation(out=X[:], in_=BANK[cb[ch]][:, :], func=AF.Exp, scale=-1.0),
                              ['bank%d' % cb[ch]], [Xk])
                        st[(ch, 'X')] = (X, Xk)
                    for ch in range(2):
                        SP, SPk = st[(ch, 'SP')]
                        tk.op('pe', lambda e: e.matmul(BANK[cb[ch]][:, :], lhsT=K['lstr'][:, :], rhs=SP[:], start=False,
                                                       stop=last), [SPk, 'c_lstr'], ['bank%d' % cb[ch]])
                    for ch in range(2):
                        E, Ek = st[(ch, 'E')]
                        X, Xk = st[(ch, 'X')]
                        W, Wk = bufs[(ch, 'W')].next()
                        tk.op('dve', lambda e: e.tensor_tensor(out=W[:], in0=E[:], in1=X[:], op=ALU.mult), [Ek, Xk], [Wk])
                        if kbr >= 0:
                            tk.op('pool', lambda e: e.tensor_tensor(out=W[:], in0=W[:], in1=mstrict[:, kbr, :],
                                                                    op=ALU.mult), [Wk, 'mstr16'], [Wk])
                        tk.op('pe', lambda e: e.matmul(BANK[ob[ch]][0:64, :], lhsT=V[:, kb, ch * 64:(ch + 1) * 64],
                                                       rhs=W[:], start=first, stop=last), [Wk, 'sbV'],
                              ['bank%d' % ob[ch]])
                for ch in range(2):
                    h = 2 * c + ch
                    o_, ok = ost.next()
                    tk.op('dve', lambda e: e.tensor_copy(out=o_[:], in_=BANK[ob[ch]][0:64, :]), ['bank%d' % ob[ch]], [ok])
                    tk.dma('pool', scr['oT'][h * 64:(h + 1) * 64, qs], o_[:], reads=[ok], writes=['d_oT'])
        drain(cx, K)


NIT_BISECT = 26


def phase_dsa(cx, K, PS, BANK, din, scr, joff_t):
    nc, tk = cx.nc, cx.tk
    S, T, NKB = cx.S, cx.T, cx.NKB
    NQB = T // 128
    with ExitStack() as es:
        kix = cx.sb(es, "kix", [128, S], BF16)
        kd = cx.sb(es, "kd", [128, S], BF16)
        vda = cx.sb(es, "vda", [128, NKB, 65], BF16)
        score = cx.sb(es, "score", [128, S], F32)
        maskT = cx.sb(es, "maskT", [128, NKB, 128], BF16)
        junk = maskT[:].rearrange("p a b -> p (a b)")
        cb512 = cx.sb(es, "cb512", [128, 512], F32)
        with ExitStack() as es2:
            vi = cx.sb(es2, "cb_vi", [128, 512], I32)
            tk.op('pool', lambda e: e.iota(vi[:], pattern=[[-128, 4], [-1, 128]], base=0, channel_multiplier=1), [],
                  ['cb_vi'])
            tk.op('dve', lambda e: e.tensor_copy(out=cb512[:], in_=vi[:]), ['cb_vi'], ['cb512'])
            tk.op('dve', lambda e: e.tensor_scalar(out=cb512[:], in0=cb512[:], scalar1=joff_t[:, 0:1], scalar2=0.0,
                                                   op0=ALU.add, op1=ALU.is_ge), ['cb512', 'joff'], ['cb512'])
            tk.op('dve', lambda e: e.tensor_scalar(out=cb512[:], in0=cb512[:], scalar1=-1.0, scalar2=-NEG,
                                                   op0=ALU.add, op1=ALU.mult), ['cb512'], ['cb512'])
            drain(cx, K)
        tk.dma('sp', kix[:], scr['kixT'][:, :], writes=['kix'])
        tk.dma('sp', kd[:], scr['kdT'][:, :], writes=['kd'])
        tk.dma('sp', vda[:, :, 0:64], scr['vd'][:, :].rearrange("(kb p) d -> p kb d", p=128), writes=['vda'])
        tk.op('pool', lambda e: e.memset(vda[:, :, 64:65], 1.0), [], ['vda1'])
        qixs = Ring([(cx.sb(es, "qix%d" % i, [128, 2, 128], BF16), "qix%d" % i) for i in range(1)])
        qds = Ring([(cx.sb(es, "qd%d" % i, [128, 4, 128], BF16), "qd%d" % i) for i in range(2)])
        wixs = Ring([(cx.sb(es, "wixt%d" % i, [128, 4], F32), "wixt%d" % i) for i in range(2)])
        tmps = Ring([(cx.sb(es, "ixtmp%d" % i, [128, 512], F32), "ixtmp%d" % i) for i in range(2)])
        mqs = Ring([(cx.sb(es, "mq%d" % i, [128, 512], BF16), "mq%d" % i) for i in range(2)])
        Es = Ring([(cx.sb(es, "dsE%d" % i, [128, 1024], BF16), "dsE%d" % i) for i in range(2)])
        Ps = Ring([(cx.sb(es, "dsP%d" % i, [128, 8, 128], BF16), "dsP%d" % i) for i in range(2)])
        sm = {nm: cx.sb(es, "bs_" + nm, [128, 1], F32) for nm in ('am', 'lo', 'w0', 'mid', 'cnt', 'p', 'hi', 'need')}
        ghs = Ring([(cx.sb(es, "gh%d" % i, [128, 512], BF16), "gh%d" % i) for i in range(1)])
        gls = Ring([(cx.sb(es, "gl%d" % i, [128, 512], BF16), "gl%d" % i) for i in range(1)])
        cums = Ring([(cx.sb(es, "cum%d" % i, [128, 512], F32), "cum%d" % i) for i in range(2)])
        zer512 = cx.sb(es, "zer512", [128, 512], BF16)
        tk.op('pool', lambda e: e.memset(zer512[:], 0.0), [], ['zer512'])
        rden = cx.sb(es, "rden", [128, 8], F32)
        otm = cx.sb(es, "otm", [128, 512], BF16)
        ostg = cx.sb(es, "dsostg", [128, 4, 128], BF16)
        lsets = Ring([(0, 1, 2, 3), (4, 5, 6, 7)])
        trb = Ring([0, 1, 2, 3])
        spair = Ring([0, 1])

        for k in range(NQB):
            nkb = 4 * k + 4
            n = nkb * 128
            qcols = slice(k * 128, (k + 1) * 128)
            qix, qixk = qixs.next()
            qd, qdk = qds.next()
            wx, wxk = wixs.next()
            for c in range(2):
                tk.dma('sp', qix[:, c, :], scr['qixT'][c][:, qcols], writes=[qixk])
            for c in range(4):
                tk.dma('sp', qd[:, c, :], scr['qdT'][c][:, qcols], writes=[qdk])
            tk.dma('sp', wx[:], scr['wix'][qcols, :], writes=[wxk])
            for kc in range(k + 1):
                ks = slice(kc * 512, (kc + 1) * 512)
                lb = lsets.next()
                for h in range(4):
                    c, pb = h // 2, 64 * (h % 2)
                    tk.op('pe', lambda e: e.matmul(BANK[lb[h]][:, :], lhsT=qix[pb:pb + 64, c, :], rhs=kix[pb:pb + 64, ks],
                                                   start=True, stop=True), [qixk, 'kix'], ['bank%d' % lb[h]])
                tk.op('dve', lambda e: e.tensor_scalar(out=score[:, ks], in0=BANK[lb[0]][:, :], scalar1=0.0,
                                                       scalar2=wx[:, 0:1], op0=ALU.max, op1=ALU.mult),
                      ['bank%d' % lb[0], wxk], ['score%d' % kc])
                for h in range(1, 4):
                    tm, tmk = tmps.next()
                    tk.op('dve', lambda e: e.tensor_scalar(out=tm[:], in0=BANK[lb[h]][:, :], scalar1=0.0,
                                                           scalar2=wx[:, h:h + 1], op0=ALU.max, op1=ALU.mult),
                          ['bank%d' % lb[h], wxk], [tmk])
                    tk.op('pool', lambda e: e.tensor_tensor(out=score[:, ks], in0=score[:, ks], in1=tm[:], op=ALU.add),
                          [tmk, 'score%d' % kc], ['score%d' % kc])
            skeys = ['score%d' % kc for kc in range(k + 1)]
            tk.op('dve', lambda e: e.tensor_reduce(out=sm['am'][:], in_=score[:, 0:n], axis=AX.X, op=ALU.max,
                                                   apply_absolute_value=True), skeys, ['bs_am'])
            lastk = 'score%d' % k
            tk.op('pool', lambda e: e.tensor_tensor(out=score[:, k * 512:(k + 1) * 512], in0=score[:, k * 512:(k + 1) * 512],
                                                    in1=cb512[:], op=ALU.add), [lastk, 'cb512', 'bs_am'], [lastk])
            tk.op('dve', lambda e: e.tensor_scalar(out=sm['lo'][:], in0=sm['am'][:], scalar1=-1.001, scalar2=-1e-6,
                                                   op0=ALU.mult, op1=ALU.add), ['bs_am'], ['bs_lo'])
            tk.op('dve', lambda e: e.tensor_scalar(out=sm['w0'][:], in0=sm['am'][:], scalar1=2.002, scalar2=2e-6,
                                                   op0=ALU.mult, op1=ALU.add), ['bs_am'], ['bs_w0'])
            for it in range(NIT_BISECT):
                f = 2.0 ** -(it + 1)
                tk.op('dve', lambda e: e.scalar_tensor_tensor(out=sm['mid'][:], in0=sm['w0'][:], scalar=f, in1=sm['lo'][:],
                                                              op0=ALU.mult, op1=ALU.add), ['bs_w0', 'bs_lo'], ['bs_mid'])
                tk.op('dve', lambda e: e.tensor_scalar(out=junk[:, 0:n], in0=score[:, 0:n], scalar1=sm['mid'][:, 0:1],
                                                       scalar2=None, op0=ALU.is_ge, op1=ALU.add, accum_out=sm['cnt'][:]),
                      skeys + ['bs_mid'], ['maskT', 'bs_cnt'])
                tk.op('dve', lambda e: e.tensor_scalar(out=sm['p'][:], in0=sm['cnt'][:], scalar1=TOPK, scalar2=f,
                                                       op0=ALU.is_ge, op1=ALU.mult), ['bs_cnt'], ['bs_p'])
                tk.op('dve', lambda e: e.scalar_tensor_tensor(out=sm['lo'][:], in0=sm['p'][:], scalar=sm['w0'][:, 0:1],
                                                              in1=sm['lo'][:], op0=ALU.mult, op1=ALU.add),
                      ['bs_p', 'bs_w0', 'bs_lo'], ['bs_lo'])
            fw = 2.0 ** -NIT_BISECT
            tk.op('dve', lambda e: e.scalar_tensor_tensor(out=sm['hi'][:], in0=sm['w0'][:], scalar=fw, in1=sm['lo'][:],
                                                          op0=ALU.mult, op1=ALU.add), ['bs_w0', 'bs_lo'], ['bs_hi'])
            tk.op('dve', lambda e: e.tensor_scalar(out=junk[:, 0:n], in0=score[:, 0:n], scalar1=sm['hi'][:, 0:1],
                                                   scalar2=None, op0=ALU.is_ge, op1=ALU.add, accum_out=sm['cnt'][:]),
                  skeys + ['bs_hi'], ['maskT', 'bs_cnt'])
            tk.op('dve', lambda e: e.tensor_scalar(out=sm['need'][:], in0=sm['cnt'][:], scalar1=-1.0, scalar2=TOPK,
                                                   op0=ALU.mult, op1=ALU.add), ['bs_cnt'], ['bs_need'])
            carry = None
            for kc in range(k + 1):
                ks = slice(kc * 512, (kc + 1) * 512)
                gh, ghk = ghs.next()
                gl, glk = gls.next()
                cum, cumk = cums.next()
                mq, mqk = mqs.next()
                tk.op('dve', lambda e: e.tensor_scalar(out=gh[:], in0=score[:, ks], scalar1=sm['hi'][:, 0:1], scalar2=None,
                                                       op0=ALU.is_ge), ['score%d' % kc, 'bs_hi'], [ghk])
                tk.op('dve', lambda e: e.tensor_scalar(out=gl[:], in0=score[:, ks], scalar1=sm['lo'][:, 0:1], scalar2=None,
                                                       op0=ALU.is_ge), ['score%d' % kc, 'bs_lo'], [glk])
                tk.op('pool', lambda e: e.tensor_tensor(out=gl[:], in0=gl[:], in1=gh[:], op=ALU.subtract), [glk, ghk], [glk])
                if carry is None:
                    tk.op('dve', lambda e: e.tensor_tensor_scan(out=cum[:], data0=gl[:], data1=zer512[:], initial=0.0,
                                                                op0=ALU.add, op1=ALU.add), [glk, 'zer512'], [cumk])
                else:
                    cprev, cprevk = carry
                    tk.op('dve', lambda e: e.tensor_tensor_scan(out=cum[:], data0=gl[:], data1=zer512[:],
                                                                initial=cprev[:, 511:512], op0=ALU.add, op1=ALU.add),
                          [glk, 'zer512', cprevk], [cumk])
                carry = (cum, cumk)
                tk.op('dve', lambda e: e.scalar_tensor_tensor(out=gl[:], in0=cum[:], scalar=sm['need'][:, 0:1], in1=gl[:],
                                                              op0=ALU.is_le, op1=ALU.mult), [cumk, 'bs_need', glk], [glk])
                tk.op('pool', lambda e: e.tensor_tensor(out=mq[:], in0=gh[:], in1=gl[:], op=ALU.add), [ghk, glk], [mqk])
                b = trb.next()
                pv = BANK[b].bitcast(BF16)
                for r in range(4):
                    tk.op('pe', lambda e: e.transpose(out=pv[:, r * 128:(r + 1) * 128], in_=mq[:, r * 128:(r + 1) * 128],
                                                      identity=K['ident'][:]), [mqk, 'c_ident'], ['bank%d' % b])
                tk.op('act', lambda e: e.copy(out=maskT[:, 4 * kc:4 * kc + 4, :].rearrange("p a b -> p (a b)"),
                                              in_=pv[:, 0:512]), ['bank%d' % b], ['maskT'])
            OB = [BANK[6], BANK[7]]
            for kb in range(nkb):
                first, last = (kb == 0), (kb == nkb - 1)
                p = spair.next()
                kcols = slice(kb * 128, (kb + 1) * 128)
                for hf in range(2):
                    pb = 64 * hf
                    tk.op('pe', lambda e: e.matmul(PS[p][:, hf * 512:(hf + 1) * 512], lhsT=kd[pb:pb + 64, kcols],
                                                   rhs=qd[pb:pb + 64, :, :].rearrange("p a b -> p (a b)"), start=True,
                                                   stop=True), ['kd', qdk], ['bank%d' % (2 * p + hf)])
                E, Ek = Es.next()
                tk.op('act', lambda e: e.activation(out=E[:], in_=PS[p][:, :], func=AF.Exp, scale=0.125),
                      ['bank%d' % (2 * p), 'bank%d' % (2 * p + 1)], [Ek])
                P, Pk = Ps.next()
                tk.op('dve', lambda e: e.tensor_tensor(out=P[:], in0=E[:].rearrange("p (a b) -> p a b", a=8),
                                                       in1=maskT[:, kb:kb + 1, :].to_broadcast([128, 8, 128]),
                                                       op=ALU.mult), [Ek, 'maskT'], [Pk])
                for g in range(8):
                    bk, sl = g // 4, g % 4
                    tk.op('pe', lambda e: e.matmul(OB[bk][:, sl * 65:(sl + 1) * 65], lhsT=P[:, g, :], rhs=vda[:, kb, :],
                                                   start=(first and sl == 0), stop=last, skip_group_check=True),
                          [Pk, 'vda', 'vda1'], ['bank%d' % (6 + bk)])
            for bk in range(2):
                view = OB[bk][:, 0:260].rearrange("p (a b) -> p a b", a=4)
                tk.op('dve', lambda e: e.reciprocal(out=rden[:, bk * 4:(bk + 1) * 4], in_=view[:, :, 64]),
                      ['bank%d' % (6 + bk)], ['rden%d' % bk])
                tk.op('dve', lambda e: e.tensor_tensor(
                    out=otm[:].rearrange("p (a t b) -> p a t b", a=4, t=2)[:, :, bk, :], in0=view[:, :, 0:64],
                    in1=rden[:, bk * 4:(bk + 1) * 4].unsqueeze(2).to_broadcast([128, 4, 64]), op=ALU.mult),
                    ['bank%d' % (6 + bk), 'rden%d' % bk], ['otm%d' % bk])
            b = trb.next()
            pv = BANK[b].bitcast(BF16)
            for c in range(4):
                tk.op('pe', lambda e: e.transpose(out=pv[:, c * 128:(c + 1) * 128], in_=otm[:, c * 128:(c + 1) * 128],
                                                  identity=K['ident'][:]), ['otm0', 'otm1', 'c_ident'], ['bank%d' % b])
            tk.op('act', lambda e: e.copy(out=ostg[:].rearrange("p a b -> p (a b)"), in_=pv[:, 0:512]), ['bank%d' % b],
                  ['dsostg'])
            tk.dma('pool', scr['oT'][512:1024, qcols].rearrange("(c p) t -> p c t", p=128), ostg[:], reads=['dsostg'])
        drain(cx, K)


def resid_ln(cx, K, xap, xkey, yps, ykeys, gb, gbkey, lng, lnb, lnkeys, W):
    tk = cx.tk
    t, tkk = W['t'], W['tk']
    junk, jk = W['junk'], W['junkk']
    s = W['small']
    tk.op('dve', lambda e: e.tensor_tensor(out=t[:], in0=yps, in1=gb[:], op=ALU.mult), list(ykeys) + [gbkey], [tkk])
    tk.op('dve', lambda e: e.scalar_tensor_tensor(out=xap, in0=xap, scalar=float(ALPHA), in1=t[:], op0=ALU.mult,
                                                  op1=ALU.add), [xkey, tkk], [xkey])
    tk.op('act', lambda e: e.activation(out=junk[:], in_=xap, func=AF.Identity, accum_out=s['sum'][:]), [xkey],
          [jk, 'ln_sum'])
    tk.op('act', lambda e: e.activation(out=junk[:], in_=xap, func=AF.Square, accum_out=s['ssq'][:]), [xkey],
          [jk, 'ln_ssq'])
    tk.op('dve', lambda e: e.tensor_scalar(out=s['mean'][:], in0=s['sum'][:], scalar1=1.0 / D, scalar2=None,
                                           op0=ALU.mult), ['ln_sum'], ['ln_mean'])
    tk.op('dve', lambda e: e.tensor_tensor(out=s['msq'][:], in0=s['mean'][:], in1=s['mean'][:], op=ALU.mult),
          ['ln_mean'], ['ln_msq'])
    tk.op('dve', lambda e: e.scalar_tensor_tensor(out=s['var'][:], in0=s['ssq'][:], scalar=1.0 / D, in1=s['msq'][:],
                                                  op0=ALU.mult, op1=ALU.subtract), ['ln_ssq', 'ln_msq'], ['ln_var'])
    tk.op('act', lambda e: e.activation(out=s['std'][:], in_=s['var'][:], func=AF.Sqrt, bias=K['eps'][:, 0:1]),
          ['ln_var', 'c_eps'], ['ln_std'])
    tk.op('dve', lambda e: e.reciprocal(out=s['rstd'][:], in_=s['std'][:]), ['ln_std'], ['ln_rstd'])
    tk.op('dve', lambda e: e.scalar_tensor_tensor(out=s['nmr'][:], in0=s['mean'][:], scalar=-1.0, in1=s['rstd'][:],
                                                  op0=ALU.mult, op1=ALU.mult), ['ln_mean', 'ln_rstd'], ['ln_nmr'])
    tk.op('act', lambda e: e.activation(out=t[:], in_=xap, func=AF.Identity, scale=s['rstd'][:, 0:1],
                                        bias=s['nmr'][:, 0:1]), [xkey, 'ln_rstd', 'ln_nmr'], [tkk])
    tk.op('dve', lambda e: e.tensor_tensor(out=t[:], in0=t[:], in1=lng[:], op=ALU.mult), [tkk] + list(lnkeys), [tkk])
    tk.op('pool', lambda e: e.tensor_tensor(out=xap, in0=t[:], in1=lnb[:], op=ALU.add), [tkk] + list(lnkeys), [xkey])


def ln_work(cx, es, tag):
    W = {'t': cx.sb(es, "lnt" + tag, [128, D], F32), 'tk': 'lnt' + tag,
         'junk': cx.sb(es, "lnjunk" + tag, [128, D], BF16), 'junkk': 'lnjunk' + tag,
         'small': {nm: cx.sb(es, "ln_%s%s" % (nm, tag), [128, 1], F32)
                   for nm in ('sum', 'ssq', 'mean', 'msq', 'var', 'std', 'rstd', 'nmr')}}
    return W


def load_bcast(cx, es, dram_row, name):
    t = cx.sb(es, name, [128, D], F32)
    cx.tk.dma('sp', t[:], dram_row[0:1, :].partition_broadcast(128), writes=[name])
    return t


def phase_outproj(cx, K, PS, din, wout_d, lng_d, lnb_d, gbm, gbkey, oT_d, x_src, x_dst):
    tk = cx.tk
    with ExitStack() as es:
        wout = cx.sb(es, "wout", [128, 8, D], BF16)
        for c in range(8):
            tk.dma('pool', wout[:, c, :], wout_d[c * 128:(c + 1) * 128, :], writes=['wout'])
        lng = load_bcast(cx, es, lng_d, "lng_a")
        lnb = load_bcast(cx, es, lnb_d, "lnb_a")
        W = ln_work(cx, es, "a")
        oTs = Ring([(cx.sb(es, "oTt%d" % i, [128, 8, 512], BF16), "oTt%d" % i) for i in range(2)])
        xts = Ring([(cx.sb(es, "xa%d" % i, [128, 4, D], F32), "xa%d" % i) for i in range(2)])
        pp = Ring([0, 1, 2, 3])
        for i in range(cx.NO):
            cols = slice(i * 512, (i + 1) * 512)
            oT, oTk = oTs.next()
            xt, xk = xts.next()
            tk.dma('sp', oT[:], oT_d[:, cols].rearrange("(c p) t -> p c t", p=128), writes=[oTk])
            tk.dma('sp', xt[:], x_src[cols, :].rearrange("(s p) d -> p s d", p=128), writes=[xk])
            for sbk in range(4):
                p = pp.next()
                for half in range(2):
                    for c in range(8):
                        tk.op('pe', lambda e: e.matmul(PS[p][:, half * 512:(half + 1) * 512],
                                                       lhsT=oT[:, c, sbk * 128:(sbk + 1) * 128],
                                                       rhs=wout[:, c, half * 512:(half + 1) * 512], start=(c == 0),
                                                       stop=(c == 7)), [oTk, 'wout'], ['bank%d' % (2 * p + half)])
                xkey = xk + "_%d" % sbk
                tk.res[xkey] = tk.res.get(xkey) or {'w': tk.res[xk]['w'], 'r': {}}
                resid_ln(cx, K, xt[:, sbk, :], xkey, PS[p][:, :], ['bank%d' % (2 * p), 'bank%d' % (2 * p + 1)], gbm, gbkey,
                         lng, lnb, ['lng_a', 'lnb_a'], W)
            keys = [xk + "_%d" % sbk for sbk in range(4)]
            tk.dma('pool', x_dst[cols, :].rearrange("(s p) d -> p s d", p=128), xt[:], reads=keys, writes=[xk])
            for kk in keys:
                tk.res.pop(kk, None)
        drain(cx, K)


def phase_ffn(cx, K, PS, BANK, wg_d, wu_d, wd_d, lng_d, lnb_d, modc, modkey, gbf, gbkey, x_src, x_dst):
    tk = cx.tk
    with ExitStack() as es:
        wg = cx.sb(es, "wg", [128, 8, DFF], BF16)
        wu = cx.sb(es, "wu", [128, 8, DFF], BF16)
        wd = cx.sb(es, "wd", [128, NFF, D], BF16)
        for c in range(8):
            tk.dma('pool', wg[:, c, :], wg_d[c * 128:(c + 1) * 128, :], writes=['wg'])
            tk.dma('pool', wu[:, c, :], wu_d[c * 128:(c + 1) * 128, :], writes=['wu'])
        for c in range(NFF):
            tk.dma('pool', wd[:, c, :], wd_d[c * 128:(c + 1) * 128, :], writes=['wd'])
        lng = load_bcast(cx, es, lng_d, "lng_f")
        lnb = load_bcast(cx, es, lnb_d, "lnb_f")
        W = ln_work(cx, es, "f")
        xts = Ring([(cx.sb(es, "xf%d" % i, [128, 2, D], F32), "xf%d" % i) for i in range(2)])
        hT = cx.sb(es, "hTf", [128, 8, 256], BF16)
        aT = cx.sb(es, "aTf", [128, NFF, 256], BF16)
        sgs = Ring([(cx.sb(es, "sg%d" % i, [128, 256], F32), "sg%d" % i) for i in range(2)])
        tb = Ring([0, 1, 2, 3])
        gub = Ring([(0, 1), (2, 3)])
        pp = Ring([2, 3])
        NTL = cx.T // 256
        for i in range(NTL):
            rows = slice(i * 256, (i + 1) * 256)
            xt, xk = xts.next()
            tk.dma('sp', xt[:], x_src[rows, :].rearrange("(s p) d -> p s d", p=128), writes=[xk])
            for sbk in range(2):
                for c4 in range(2):
                    b = tb.next()
                    for cc in range(4):
                        c = c4 * 4 + cc
                        tk.op('pe', lambda e: e.transpose(out=BANK[b][:, cc * 128:(cc + 1) * 128],
                                                          in_=xt[:, sbk, c * 128:(c + 1) * 128], identity=K['identf'][:]),
                              [xk, 'c_identf'], ['bank%d' % b])
                    for cc in range(4):
                        c = c4 * 4 + cc
                        tk.op('act', lambda e: e.activation(out=hT[:, c, sbk * 128:(sbk + 1) * 128],
                                                            in_=BANK[b][:, cc * 128:(cc + 1) * 128], func=AF.Identity,
                                                            scale=modc[:, 32 + c:33 + c], bias=modc[:, 24 + c:25 + c]),
                              ['bank%d' % b, modkey], ['hTf'])
            for f in range(NFF):
                bg, bu = gub.next()
                for c in range(8):
                    tk.op('pe', lambda e: e.matmul(BANK[bg][:, 0:256], lhsT=wg[:, c, f * 128:(f + 1) * 128], rhs=hT[:, c, :],
                                                   start=(c == 0), stop=(c == 7)), ['wg', 'hTf'], ['bank%d' % bg])
                for c in range(8):
                    tk.op('pe', lambda e: e.matmul(BANK[bu][:, 0:256], lhsT=wu[:, c, f * 128:(f + 1) * 128], rhs=hT[:, c, :],
                                                   start=(c == 0), stop=(c == 7)), ['wu', 'hTf'], ['bank%d' % bu])
                sg, sgk = sgs.next()
                tk.op('act', lambda e: e.activation(out=sg[:], in_=BANK[bg][:, 0:256], func=AF.Silu), ['bank%d' % bg], [sgk])
                tk.op('dve', lambda e: e.tensor_tensor(out=aT[:, f, :], in0=BANK[bu][:, 0:256], in1=sg[:], op=ALU.mult),
                      ['bank%d' % bu, sgk], ['aTf'])
            for sbk in range(2):
                p = pp.next()
                for half in range(2):
                    for f in range(NFF):
                        tk.op('pe', lambda e: e.matmul(PS[p][:, half * 512:(half + 1) * 512],
                                                       lhsT=aT[:, f, sbk * 128:(sbk + 1) * 128],
                                                       rhs=wd[:, f, half * 512:(half + 1) * 512], start=(f == 0),
                                                       stop=(f == NFF - 1)), ['aTf', 'wd'], ['bank%d' % (2 * p + half)])
                xkey = xk + "_%d" % sbk
                tk.res[xkey] = {'w': tk.res[xk]['w'], 'r': dict(tk.res[xk]['r'])}
                resid_ln(cx, K, xt[:, sbk, :], xkey, PS[p][:, :], ['bank%d' % (2 * p), 'bank%d' % (2 * p + 1)], gbf, gbkey,
                         lng, lnb, ['lng_f', 'lnb_f'], W)
            keys = [xk + "_%d" % sbk for sbk in range(2)]
            tk.dma('pool', x_dst[rows, :].rearrange("(s p) d -> p s d", p=128), xt[:], reads=keys, writes=[xk])
            for kk in keys:
                tk.res.pop(kk, None)
        drain(cx, K)


def phase_proj1(cx, K, BANK, din, modc, modkey, x_src, dout):
    tk = cx.tk
    with ExitStack() as es:
        w1 = cx.sb(es, "w1", [128, 8, 5120], BF16)
        for c in range(8):
            tk.dma('pool', w1[:, c, :], din['w1'][c * 128:(c + 1) * 128, :], writes=['w1'])
        xts = Ring([(cx.sb(es, "xp%d" % i, [128, 4, D], F32), "xp%d" % i) for i in range(2)])
        hT = cx.sb(es, "hTp", [128, 8, 512], BF16)
        posi = cx.sb(es, "posi1", [128, 512], I32)
        posf = cx.sb(es, "posf1", [128, 512], F32)
        cosF = cx.sb(es, "cosFp1", [128, 512], F32)
        sinS = cx.sb(es, "sinSp1", [128, 512], F32)
        tmp = {'ang': cx.sb(es, "rt1_ang", [128, 512], F32), 'ki': cx.sb(es, "rt1_ki", [128, 512], I32),
               'kf': cx.sb(es, "rt1_kf", [128, 512], F32), 'r': cx.sb(es, "rt1_r", [128, 512], F32)}
        stg = Ring([(cx.sb(es, "stgp%d" % i, [128, 512], BF16), 'stgp%d' % i) for i in range(4)])
        t1s = Ring([(cx.sb(es, "rp1t1_%d" % i, [128, 512], F32), 'rp1t1_%d' % i) for i in range(2)])
        t2s = Ring([(cx.sb(es, "rp1t2_%d" % i, [128, 512], F32), 'rp1t2_%d' % i) for i in range(2)])
        kpairs = Ring([(0, 1), (2, 3)])
        tb = Ring([4, 5, 6, 7])
        vb = Ring([4, 5, 6, 7])
        for i in range(cx.NO):
            cols = slice(i * 512, (i + 1) * 512)
            xt, xk = xts.next()
            tk.dma('sp', xt[:], x_src[cols, :].rearrange("(s p) d -> p s d", p=128), writes=[xk])
            tk.dma('sp', posi[:], din['posown'][0:1, cols].partition_broadcast(128), writes=['posi1'])
            tk.op('dve', lambda e: e.tensor_copy(out=posf[:], in_=posi[:]), ['posi1'], ['posf1'])
            rope_tables(cx, K, posf[:], 'posf1', cosF[:], sinS[:], 'cos1', 'sin1', tmp)
            for sbk in range(4):
                for c4 in range(2):
                    b = tb.next()
                    for cc in range(4):
                        c = c4 * 4 + cc
                        tk.op('pe', lambda e: e.transpose(out=BANK[b][:, cc * 128:(cc + 1) * 128],
                                                          in_=xt[:, sbk, c * 128:(c + 1) * 128], identity=K['identf'][:]),
                              [xk, 'c_identf'], ['bank%d' % b])
                    for cc in range(4):
                        c = c4 * 4 + cc
                        tk.op('act', lambda e: e.activation(out=hT[:, c, sbk * 128:(sbk + 1) * 128],
                                                            in_=BANK[b][:, cc * 128:(cc + 1) * 128], func=AF.Identity,
                                                            scale=modc[:, 8 + c:9 + c], bias=modc[:, c:c + 1]),
                              ['bank%d' % b, modkey], ['hTp'])
            for (base, dst) in ((0, dout['q1T']), (2048, dout.get('k1T'))):
                for c in range(8):
                    b0, b1 = kpairs.next()
                    for (bb, col0) in ((b0, base + c * 128), (b1, base + 1024 + c * 128)):
                        for k in range(8):
                            tk.op('pe', lambda e: e.matmul(BANK[bb][:, :], lhsT=w1[:, k, col0:col0 + 128], rhs=hT[:, k, :],
                                                           start=(k == 0), stop=(k == 7)), ['w1', 'hTp'], ['bank%d' % bb])
                    t1, k1 = t1s.next()
                    t2, k2 = t2s.next()
                    tk.op('dve', lambda e: e.tensor_tensor(out=t1[:], in0=BANK[b0][:, :], in1=cosF[:], op=ALU.mult),
                          ['bank%d' % b0, 'cos1'], [k1])
                    tk.op('dve', lambda e: e.tensor_tensor(out=t2[:], in0=BANK[b1][:, :], in1=sinS[:], op=ALU.mult),
                          ['bank%d' % b1, 'sin1'], [k2])
                    st, sk = stg.next()
                    tk.op('pool', lambda e: e.tensor_tensor(out=st[:], in0=t1[:], in1=t2[:], op=ALU.add), [k1, k2], [sk])
                    if base == 2048 and 'k1T_h' in dout:
                        tk.dma('pool', dout['k1T_h'][c][:, cols], st[:], reads=[sk])
                    else:
                        tk.dma('pool', dst[c * 128:(c + 1) * 128, cols], st[:], reads=[sk])
            for sbk in range(4):
                rows = slice(i * 512 + sbk * 128, i * 512 + (sbk + 1) * 128)
                for half in range(2):
                    b = vb.next()
                    for k in range(8):
                        tk.op('pe', lambda e: e.matmul(BANK[b][:, :], lhsT=hT[:, k, sbk * 128:(sbk + 1) * 128],
                                                       rhs=w1[:, k, 4096 + half * 512:4096 + (half + 1) * 512],
                                                       start=(k == 0), stop=(k == 7)), ['w1', 'hTp'], ['bank%d' % b])
                    st, sk = stg.next()
                    tk.op('act', lambda e: e.copy(out=st[:], in_=BANK[b][:, :]), ['bank%d' % b], [sk])
                    if 'v1_h' in dout:
                        for hh in range(4):
                            tk.dma('pool', dout['v1_h'][half * 4 + hh][rows, :], st[:, hh * 128:(hh + 1) * 128], reads=[sk])
                    else:
                        tk.dma('pool', dout['v1'][rows, half * 512:(half + 1) * 512], st[:], reads=[sk])
        drain(cx, K)


def phase_diff(cx, K, BANK, din, scr, joff_t):
    tk = cx.tk
    S, T, NO, NKB = cx.S, cx.T, cx.NO, cx.NKB
    with ExitStack() as es:
        mincl = make_tmasks(cx, es, joff_t, False, "mincl16")
        lam = {nm: cx.sb(es, "lam_" + nm, [128, 64], F32) for nm in ('q1', 'k1', 'q2', 'k2')}
        lsm = {nm: cx.sb(es, "lams_" + nm, [128, 1], F32) for nm in ('s1', 's2', 'e1', 'e2', 'nl')}
        gsub = cx.sb(es, "gsub", [128, 128], F32)
        for nm in ('q1', 'k1', 'q2', 'k2'):
            tk.dma('sp', lam[nm][:], din['lam_' + nm][0:1, :].partition_broadcast(128), writes=['lam_' + nm])
        tk.dma('sp', gsub[:], din['subln_g'][0:1, :].partition_broadcast(128), writes=['gsub'])
        tk.op('dve', lambda e: e.tensor_scalar(out=gsub[:], in0=gsub[:], scalar1=float(1.0 - LAMBDA_INIT), scalar2=None,
                                               op0=ALU.mult), ['gsub'], ['gsub'])
        for a, s_, e_ in (('1', 's1', 'e1'), ('2', 's2', 'e2')):
            tk.op('dve', lambda e: e.tensor_tensor(out=lam['q' + a][:], in0=lam['q' + a][:], in1=lam['k' + a][:],
                                                   op=ALU.mult), ['lam_q' + a, 'lam_k' + a], ['lam_q' + a])
            tk.op('dve', lambda e: e.tensor_reduce(out=lsm[s_][:], in_=lam['q' + a][:], axis=AX.X, op=ALU.add),
                  ['lam_q' + a], ['lams_' + s_])
            tk.op('act', lambda e: e.activation(out=lsm[e_][:], in_=lsm[s_][:], func=AF.Exp), ['lams_' + s_],
                  ['lams_' + e_])
        tk.op('dve', lambda e: e.tensor_tensor(out=lsm['nl'][:], in0=lsm['e2'][:], in1=lsm['e1'][:], op=ALU.subtract),
              ['lams_e1', 'lams_e2'], ['lams_nl'])
        tk.op('dve', lambda e: e.tensor_scalar(out=lsm['nl'][:], in0=lsm['nl'][:], scalar1=float(-LAMBDA_INIT),
                                               scalar2=None, op0=ALU.add), ['lams_nl'], ['lams_nl'])
        KT = cx.sb(es, "dfKT", [128, S], BF16)
        V = cx.sb(es, "dfV", [128, NKB, 128], BF16)
        QT = cx.sb(es, "dfQT", [128, T], BF16)
        Pb = {ch: Ring([(cx.sb(es, "dfP%d_%d" % (ch, i), [128, 512], BF16), "dfP%d_%d" % (ch, i)) for i in range(3)])
              for ch in range(2)}
        rd = cx.sb(es, "dfrd", [128, 2, 4], F32)
        t2 = cx.sb(es, "dft2", [128, 128], F32)
        av = cx.sb(es, "dfa", [128, 128], F32)
        junk = cx.sb(es, "dfjunk", [128, 128], BF16)
        ssq = cx.sb(es, "dfssq", [128, 1], F32)
        rstd = cx.sb(es, "dfrstd", [128, 1], F32)
        otm = cx.sb(es, "dfotm", [128, 4, 128], BF16)
        ostg = Ring([(cx.sb(es, "dfostg%d" % i, [128, 512], BF16), "dfostg%d" % i) for i in range(2)])
        sb_ = {0: Ring([0, 1]), 1: Ring([4, 5])}
        ob = {0: 2, 1: 6}
        db = {0: 3, 1: 7}
        for h in range(8):
            if 'kg' in din:
                for r in range(4):
                    tk.dma('sp', KT[:].rearrange("f (k r p) -> f k r p", r=4, p=128)[:, :, r, :],
                           din['kg'][h][r * 128:(r + 1) * 128, :].rearrange("f (k p) -> f k p", p=128),
                           reads=['ag'], writes=['dfKT'])
                    tk.dma('sp', V[:].rearrange("p (k r) d -> p k r d", r=4)[:, :, r, :],
                           din['vg'][h][r * T:(r + 1) * T, :].rearrange("(k p) d -> p k d", p=128),
                           reads=['ag'], writes=['dfV'])
            else:
                tk.dma('sp', KT[:], din['k1T'][h * 128:(h + 1) * 128, :], writes=['dfKT'])
                tk.dma('sp', V[:], din['v1'][:, h * 128:(h + 1) * 128].rearrange("(kb p) d -> p kb d", p=128),
                       writes=['dfV'])
            tk.dma('sp', QT[:], din['q1T'][h * 128:(h + 1) * 128, :], writes=['dfQT'])
            for i in range(NO):
                nkb = 16 * (i + 1)
                qs = slice(i * 512, (i + 1) * 512)
                zb = {}

                def mm_s(ch, kb):
                    pb = 64 * ch
                    b = sb_[ch].next()
                    tk.op('pe', lambda e: e.matmul(BANK[b][:, :], lhsT=KT[pb:pb + 64, kb * 128:(kb + 1) * 128],
                                                   rhs=QT[pb:pb + 64, qs], start=True, stop=True),
                          ['dfKT', 'dfQT'], ['bank%d' % b])
                    zb[(ch, kb)] = b
                for ch in range(2):
                    mm_s(ch, 0)
                for kb in range(nkb):
                    kbr = kb - 16 * i
                    first, last = (kb == 0), (kb == nkb - 1)
                    cur = {}
                    for ch in range(2):
                        b = zb.pop((ch, kb))
                        P, Pk = Pb[ch].next()
                        tk.op('act', lambda e: e.activation(out=P[:], in_=BANK[b][:, :], func=AF.Exp, scale=0.125),
                              ['bank%d' % b], [Pk])
                        if kbr >= 0:
                            tk.op('pool', lambda e: e.tensor_tensor(out=P[:], in0=P[:], in1=mincl[:, kbr, :], op=ALU.mult),
                                  [Pk, 'mincl16'], [Pk])
                        cur[ch] = (P, Pk)
                    if not last:
                        for ch in range(2):
                            mm_s(ch, kb + 1)
                    for ch in range(2):
                        P, Pk = cur[ch]
                        for r in range(4):
                            tk.op('pe', lambda e: e.matmul(BANK[ob[ch]][:, r * 128:(r + 1) * 128],
                                                           lhsT=P[:, r * 128:(r + 1) * 128], rhs=V[:, kb, :],
                                                           start=(first and r == 0), stop=last, skip_group_check=True),
                                  [Pk, 'dfV'], ['bank%d' % ob[ch]])
                        for r in range(4):
                            tk.op('pe', lambda e: e.matmul(BANK[db[ch]][:, r:r + 1], lhsT=P[:, r * 128:(r + 1) * 128],
                                                           rhs=K['onesb'][:, 0:1], start=(first and r == 0), stop=last,
                                                           skip_group_check=True), [Pk, 'c_onesb'], ['bank%d' % db[ch]])
                for ch in range(2):
                    tk.op('dve', lambda e: e.reciprocal(out=rd[:, ch, :], in_=BANK[db[ch]][:, 0:4]), ['bank%d' % db[ch]],
                          ['dfrd'])
                tk.op('dve', lambda e: e.tensor_scalar(out=rd[:, 1, :], in0=rd[:, 1, :], scalar1=lsm['nl'][:, 0:1],
                                                       scalar2=None, op0=ALU.mult), ['dfrd', 'lams_nl'], ['dfrd'])
                for r in range(4):
                    tk.op('dve', lambda e: e.tensor_scalar(out=t2[:], in0=BANK[ob[1]][:, r * 128:(r + 1) * 128],
                                                           scalar1=rd[:, 1, r:r + 1], scalar2=None, op0=ALU.mult),
                          ['bank%d' % ob[1], 'dfrd'], ['dft2'])
                    tk.op('dve', lambda e: e.scalar_tensor_tensor(out=av[:], in0=BANK[ob[0]][:, r * 128:(r + 1) * 128],
                                                                  scalar=rd[:, 0, r:r + 1], in1=t2[:], op0=ALU.mult,
                                                                  op1=ALU.add), ['bank%d' % ob[0], 'dfrd', 'dft2'], ['dfa'])
                    tk.op('act', lambda e: e.activation(out=junk[:], in_=av[:], func=AF.Square, accum_out=ssq[:]), ['dfa'],
                          ['dfjunk', 'dfssq'])
                    tk.op('dve', lambda e: e.tensor_scalar(out=ssq[:], in0=ssq[:], scalar1=1.0 / 128.0, scalar2=None,
                                                           op0=ALU.mult), ['dfssq'], ['dfssq'])
                    tk.op('act', lambda e: e.activation(out=rstd[:], in_=ssq[:], func=AF.Sqrt, bias=K['eps'][:, 0:1]),
                          ['dfssq', 'c_eps'], ['dfrstd'])
                    tk.op('dve', lambda e: e.reciprocal(out=rstd[:], in_=rstd[:]), ['dfrstd'], ['dfrstd'])
                    tk.op('dve', lambda e: e.scalar_tensor_tensor(out=otm[:, r, :], in0=av[:], scalar=rstd[:, 0:1],
                                                                  in1=gsub[:], op0=ALU.mult, op1=ALU.mult),
                          ['dfa', 'dfrstd', 'gsub'], ['dfotm'])
                b = sb_[0].next()
                pv = BANK[b].bitcast(BF16)
                for r in range(4):
                    tk.op('pe', lambda e: e.transpose(out=pv[:, r * 128:(r + 1) * 128], in_=otm[:, r, :],
                                                      identity=K['ident'][:]), ['dfotm', 'c_ident'], ['bank%d' % b])
                og, ogk = ostg.next()
                tk.op('act', lambda e: e.copy(out=og[:], in_=pv[:, 0:512]), ['bank%d' % b], [ogk])
                tk.dma('pool', scr['oT'][h * 128:(h + 1) * 128, qs], og[:], reads=[ogk])
        drain(cx, K)


def build_B(S, stop_after=99, dbg=False):
    cx = Ctx(S)
    nc, tk, es = cx.nc, cx.tk, cx.es
    T = cx.T
    din = {}
    din['xown'] = cx.din("xown", [T, D], F32)
    din['q1T'] = cx.din("q1T", [D, T], BF16)
    din['k1T'] = cx.din("k1T", [D, S], BF16)
    din['v1'] = cx.din("v1", [S, D], BF16)
    din['ccol'] = cx.din("ccol", [128, 8], F32)
    din['inv'] = cx.din("inv", [128, 1], F32)
    din['sgn'] = cx.din("sgn", [128, 1], F32)
    din['joff'] = cx.din("joff", [128, 1], F32)
    din['wmod1'] = cx.din("wmod1", [D, 6 * D], F32)
    din['bmod1'] = cx.din("bmod1", [1, 6 * D], F32)
    din['wout'] = cx.din("wout", [D, D], F32)
    din['wgate'] = cx.din("wgate", [D, DFF], F32)
    din['wup'] = cx.din("wup", [D, DFF], F32)
    din['wdown'] = cx.din("wdown", [DFF, D], F32)
    for nm in ('lnmg', 'lnmb', 'lnfg', 'lnfb'):
        din[nm] = cx.din(nm, [1, D], F32)
    for nm in ('q1', 'k1', 'q2', 'k2'):
        din['lam_' + nm] = cx.din("lam_" + nm, [1, 64], F32)
    din['subln_g'] = cx.din("subln_g", [1, 128], F32)
    scr = {'oT': cx.dscr("oT", [D, T], BF16, dbg), 'xmix': cx.dscr("xmix", [T, D], F32, dbg)}
    dout = {'out': cx.dout("out", [T, D], F32)}
    with es:
        PS = [cx.ps(es, "PS%d" % i, [128, 1024], F32) for i in range(4)]
        BANK = [PS[i // 2][:, (i % 2) * 512:(i % 2 + 1) * 512] for i in range(8)]
        K = make_consts(cx, es, din)
        modc1, gbm1, gbf1 = compute_mod(cx, es, K, BANK, din['ccol'], din['wmod1'], din['bmod1'], '1')
        joff_t = cx.sb(es, "joff_t", [128, 1], F32)
        tk.dma('sp', joff_t[:], din['joff'][:, :], writes=['joff'])
        drain(cx, K)
        if stop_after >= 1:
            phase_diff(cx, K, BANK, din, scr, joff_t)
        if stop_after >= 2:
            phase_outproj(cx, K, PS, din, din['wout'], din['lnmg'], din['lnmb'], gbm1, 'gbm1', scr['oT'], din['xown'],
                          scr['xmix'])
            phase_ffn(cx, K, PS, BANK, din['wgate'], din['wup'], din['wdown'], din['lnfg'], din['lnfb'], modc1, 'modc1',
                      gbf1, 'gbf1', scr['xmix'], dout['out'])
        tk.finish('sp')
    return cx


def prep_B(inp, S, resA):
    import ml_dtypes
    c = np.asarray(inp['c'], dtype=np.float32)
    invc, sgn = rope_consts()
    common = dict(inv=invc, sgn=sgn,
                  wmod1=np.ascontiguousarray(inp['w_mod'][1], dtype=np.float32),
                  bmod1=np.ascontiguousarray(inp['b_mod'][1:2], dtype=np.float32),
                  wout=np.ascontiguousarray(inp['w_out_odd'][0], dtype=np.float32),
                  wgate=np.ascontiguousarray(inp['w_gate'][1], dtype=np.float32),
                  wup=np.ascontiguousarray(inp['w_up'][1], dtype=np.float32),
                  wdown=np.ascontiguousarray(inp['w_down'][1], dtype=np.float32),
                  lnmg=np.asarray(inp['ln_mix_g'][1:2], dtype=np.float32), lnmb=np.asarray(inp['ln_mix_b'][1:2], dtype=np.float32),
                  lnfg=np.asarray(inp['ln_ffn_g'][1:2], dtype=np.float32), lnfb=np.asarray(inp['ln_ffn_b'][1:2], dtype=np.float32),
                  lam_q1=np.asarray(inp['lam_q1'][0:1], dtype=np.float32), lam_k1=np.asarray(inp['lam_k1'][0:1], dtype=np.float32),
                  lam_q2=np.asarray(inp['lam_q2'][0:1], dtype=np.float32), lam_k2=np.asarray(inp['lam_k2'][0:1], dtype=np.float32),
                  subln_g=np.asarray(inp['subln_g'][0:1], dtype=np.float32))
    kfull, vfull = [], []
    for b in range(2):
        kT = np.zeros((D, S), dtype=ml_dtypes.bfloat16)
        v = np.zeros((S, D), dtype=ml_dtypes.bfloat16)
        for j in range(4):
            oi = own_index(j, S)
            r = resA[4 * b + j]
            kT[:, oi] = np.asarray(r['k1T'])
            v[oi, :] = np.asarray(r['v1'])
        kfull.append(kT)
        vfull.append(v)
    maps = []
    for core in range(8):
        b, j = core // 4, core % 4
        r = resA[core]
        m = dict(common)
        m['xown'] = np.ascontiguousarray(r['x1'])
        m['q1T'] = np.ascontiguousarray(r['q1T'])
        m['k1T'] = kfull[b]
        m['v1'] = vfull[b]
        m['ccol'] = np.ascontiguousarray(c[b].reshape(8, 128).T)
        m['joff'] = np.full((128, 1), 128.0 * j, dtype=np.float32)
        maps.append(m)
    return maps


SEQ = 16384
_CACHE = {}


def kernel_unfused(**inputs):
    S = SEQ
    if 'A' not in _CACHE:
        _CACHE['A'] = build_A(S)
        _CACHE['B'] = build_B(S)
    mapsA = prep_A(inputs, S)
    resA = run_bass_kernel_spmd(_CACHE['A'].nc, mapsA, core_ids=list(range(8))).results
    mapsB = prep_B(inputs, S, resA)
    del mapsA
    resB = run_bass_kernel_spmd(_CACHE['B'].nc, mapsB, core_ids=list(range(8))).results
    out = np.zeros((2, S, D), dtype=np.float32)
    for core in range(8):
        b, j = core // 4, core % 4
        out[b, own_index(j, S), :] = np.asarray(resB[core]['out'])
    return out


def build_F(S):
    cx = Ctx(S)
    nc, tk, es = cx.nc, cx.tk, cx.es
    T = cx.T
    din = {}
    din['xT'] = cx.din("xT", [D, S], F32)
    din['xown'] = cx.din("xown", [T, D], F32)
    din['xTown'] = cx.din("xTown", [D, T], F32)
    din['joff'] = cx.din("joff", [128, 1], F32)
    din['pos'] = cx.din("pos", [1, S], I32)
    din['posown'] = cx.din("posown", [1, T], I32)
    din['ccol'] = cx.din("ccol", [128, 8], F32)
    din['inv'] = cx.din("inv", [128, 1], F32)
    din['sgn'] = cx.din("sgn", [128, 1], F32)
    for l in range(2):
        din['wmod%d' % l] = cx.din("wmod%d" % l, [D, 6 * D], F32)
        din['bmod%d' % l] = cx.din("bmod%d" % l, [1, 6 * D], F32)
    din['wK'] = cx.din("wK", [D, 1024], F32)
    din['wQ'] = cx.din("wQ", [D, 2048], F32)
    din['wV'] = cx.din("wV", [D, 580], F32)
    din['w1'] = cx.din("w1", [D, 5120], F32)
    for l in ('', '1'):
        din['wout' + l] = cx.din("wout" + l, [D, D], F32)
        din['wgate' + l] = cx.din("wgate" + l, [D, DFF], F32)
        din['wup' + l] = cx.din("wup" + l, [D, DFF], F32)
        din['wdown' + l] = cx.din("wdown" + l, [DFF, D], F32)
        for nm in ('lnmg', 'lnmb', 'lnfg', 'lnfb'):
            din[nm + l] = cx.din(nm + l, [1, D], F32)
    for nm in ('q1', 'k1', 'q2', 'k2'):
        din['lam_' + nm] = cx.din("lam_" + nm, [1, 64], F32)
    din['subln_g'] = cx.din("subln_g", [1, 128], F32)
    scr = {}
    scr['ksbT'] = [cx.dscr("ksbT%d" % c, [128, S], BF16) for c in range(4)]
    scr['kdT'] = cx.dscr("kdT", [128, S], BF16)
    scr['kixT'] = cx.dscr("kixT", [128, S], BF16)
    scr['vsb'] = cx.dscr("vsb", [S, 512], BF16)
    scr['vd'] = cx.dscr("vd", [S, 64], BF16)
    scr['qsbT'] = [cx.dscr("qsbT%d" % c, [128, T], BF16) for c in range(4)]
    scr['qdT'] = [cx.dscr("qdT%d" % c, [128, T], BF16) for c in range(4)]
    scr['qixT'] = [cx.dscr("qixT%d" % c, [128, T], BF16) for c in range(2)]
    scr['wix'] = cx.dscr("wix", [T, 4], F32)
    scr['oT'] = cx.dscr("oT", [D, T], BF16)
    scr['xmix'] = cx.dscr("xmix", [T, D], F32)
    mid = {'x1': cx.dscr("x1", [T, D], F32), 'q1T': cx.dscr("q1T", [D, T], BF16),
           'k1T_h': [cx.dscr("k1T_h%d" % h, [128, T], BF16) for h in range(8)],
           'v1_h': [cx.dscr("v1_h%d" % h, [T, 128], BF16) for h in range(8)]}
    kg = [cx.dscr("kg%d" % h, [4 * 128, T], BF16) for h in range(8)]
    vg = [cx.dscr("vg%d" % h, [4 * T, 128], BF16) for h in range(8)]
    out = cx.dout("out", [T, D], F32)
    with es:
        PS = [cx.ps(es, "PS%d" % i, [128, 1024], F32) for i in range(4)]
        BANK = [PS[i // 2][:, (i % 2) * 512:(i % 2 + 1) * 512] for i in range(8)]
        K = make_consts(cx, es, din)
        modc0, _, _ = compute_mod(cx, es, K, BANK, din['ccol'], din['wmod0'], din['bmod0'], '0', want_bcast=False)
        modc1, _, _ = compute_mod(cx, es, K, BANK, din['ccol'], din['wmod1'], din['bmod1'], '1', want_bcast=False)
        joff_t = cx.sb(es, "joff_t", [128, 1], F32)
        tk.dma('sp', joff_t[:], din['joff'][:, :], writes=['joff'])
        drain(cx, K)
        phase_proj0(cx, K, BANK, din, scr, modc0)
        phase_sb(cx, K, BANK, din, scr, joff_t)
        phase_dsa(cx, K, PS, BANK, din, scr, joff_t)
        with ExitStack() as es2:
            _, gbm0, gbf0 = compute_mod(cx, es2, K, BANK, din['ccol'], din['wmod0'], din['bmod0'], '0b', want_cols=False)
            phase_outproj(cx, K, PS, din, din['wout'], din['lnmg'], din['lnmb'], gbm0, 'gbm0b', scr['oT'], din['xown'],
                          scr['xmix'])
            phase_ffn(cx, K, PS, BANK, din['wgate'], din['wup'], din['wdown'], din['lnfg'], din['lnfb'], modc0, 'modc0',
                      gbf0, 'gbf0b', scr['xmix'], mid['x1'])
        phase_proj1(cx, K, BANK, din, modc1, 'modc1', mid['x1'], mid)
        for h in range(8):
            for nm, src, dst in (('k', mid['k1T_h'][h], kg[h]), ('v', mid['v1_h'][h], vg[h])):
                inst = nc.gpsimd.collective_compute("AllGather", ALU.bypass, replica_groups=[[0, 1, 2, 3], [4, 5, 6, 7]],
                                                    ins=[src.opt()], outs=[dst.opt()])
                key = "cc_%s%d" % (nm, h)
                tk.sems[key] = es.enter_context(nc.semaphore(key))
                tk.cnt[key] = 1
                inst.then_inc(tk.sems[key])
                tk._commit((key, 1), [], ['ag'])
        drain(cx, K)
        dinB = {'q1T': mid['q1T'], 'kg': kg, 'vg': vg, 'subln_g': din['subln_g']}
        for nm in ('q1', 'k1', 'q2', 'k2'):
            dinB['lam_' + nm] = din['lam_' + nm]
        phase_diff(cx, K, BANK, dinB, scr, joff_t)
        with ExitStack() as es2:
            _, gbm1, gbf1 = compute_mod(cx, es2, K, BANK, din['ccol'], din['wmod1'], din['bmod1'], '1b', want_cols=False)
            phase_outproj(cx, K, PS, din, din['wout1'], din['lnmg1'], din['lnmb1'], gbm1, 'gbm1b', scr['oT'], mid['x1'],
                          scr['xmix'])
            phase_ffn(cx, K, PS, BANK, din['wgate1'], din['wup1'], din['wdown1'], din['lnfg1'], din['lnfb1'], modc1,
                      'modc1', gbf1, 'gbf1b', scr['xmix'], out)
        tk.finish('sp')
    return cx


def prep_F(inp, S):
    maps = prep_A(inp, S)
    extra = dict(wout1=np.ascontiguousarray(inp['w_out_odd'][0], dtype=np.float32),
                 wgate1=np.ascontiguousarray(inp['w_gate'][1], dtype=np.float32),
                 wup1=np.ascontiguousarray(inp['w_up'][1], dtype=np.float32),
                 wdown1=np.ascontiguousarray(inp['w_down'][1], dtype=np.float32),
                 lnmg1=np.asarray(inp['ln_mix_g'][1:2], dtype=np.float32), lnmb1=np.asarray(inp['ln_mix_b'][1:2], dtype=np.float32),
                 lnfg1=np.asarray(inp['ln_ffn_g'][1:2], dtype=np.float32), lnfb1=np.asarray(inp['ln_ffn_b'][1:2], dtype=np.float32),
                 lam_q1=np.asarray(inp['lam_q1'][0:1], dtype=np.float32), lam_k1=np.asarray(inp['lam_k1'][0:1], dtype=np.float32),
                 lam_q2=np.asarray(inp['lam_q2'][0:1], dtype=np.float32), lam_k2=np.asarray(inp['lam_k2'][0:1], dtype=np.float32),
                 subln_g=np.asarray(inp['subln_g'][0:1], dtype=np.float32))
    for m in maps:
        m.update(extra)
    return maps


def kernel_fused(S=SEQ, **inputs):
    if ('F', S) not in _CACHE:
        _CACHE[('F', S)] = build_F(S)
    maps = prep_F(inputs, S)
    res = run_bass_kernel_spmd(_CACHE[('F', S)].nc, maps, core_ids=list(range(8))).results
    out = np.zeros((2, S, D), dtype=np.float32)
    for core in range(8):
        b, j = core // 4, core % 4
        out[b, own_index(j, S), :] = np.asarray(res[core]['out'])
    return out


def kernel(**inputs):
    return kernel_fused(S=SEQ, **inputs)
```
